# Optimizing a Trainium2 kernel written in Bass

```python
import jax, jax.numpy as jnp
from jax import lax
import numpy as np

D_MODEL = 2048
BATCH = 4
SEQ = 8192
DEPTH = 1

HEAD_DIM = 64
SWA_Q_HEADS = 16
SWA_KV_HEADS = 2
SWA_GROUP = SWA_Q_HEADS // SWA_KV_HEADS
WINDOW = 128
FOX_HEADS = 16
FOX_BLOCK = 128
D_FF = 4 * D_MODEL
ROPE_THETA = 10000.0
RMS_EPS = 1e-6

SWA_Q_W = SWA_Q_HEADS * HEAD_DIM
SWA_KV_W = SWA_KV_HEADS * HEAD_DIM
FOX_W = FOX_HEADS * HEAD_DIM
IN_WIDTHS = (SWA_Q_W, SWA_KV_W, SWA_KV_W, FOX_W, FOX_W, FOX_W, FOX_HEADS, D_MODEL, D_MODEL)
D_IN = sum(IN_WIDTHS)
IN_SPLITS = tuple(int(v) for v in np.cumsum(IN_WIDTHS)[:-1])

kernel_name = "hybrid_swa_sink_fox_gated_block"


def rmsnorm(x, gain):
    xf = x.astype(jnp.float32)
    out = xf * lax.rsqrt(jnp.mean(xf * xf, axis=-1, keepdims=True) + RMS_EPS) * gain.astype(jnp.float32)
    return out.astype(x.dtype)


def apply_rope(t, positions):
    inv_freq = ROPE_THETA ** (-jnp.arange(0, HEAD_DIM, 2, dtype=jnp.float32) / HEAD_DIM)
    ang = positions.astype(jnp.float32)[..., None] * inv_freq
    cos = jnp.cos(ang)[:, :, None, :]
    sin = jnp.sin(ang)[:, :, None, :]
    tf = t.astype(jnp.float32)
    t1, t2 = tf[..., : HEAD_DIM // 2], tf[..., HEAD_DIM // 2:]
    out = jnp.concatenate([t1 * cos - t2 * sin, t2 * cos + t1 * sin], axis=-1)
    return out.astype(t.dtype)


def sliding_window_gqa_sinks(q, k, v, sinks):
    B, S = q.shape[0], q.shape[1]
    nb = S // WINDOW
    scale = HEAD_DIM ** -0.5
    qb = q.reshape(B, nb, WINDOW, SWA_KV_HEADS, SWA_GROUP, HEAD_DIM)
    kb = k.reshape(B, nb, WINDOW, SWA_KV_HEADS, HEAD_DIM)
    vb = v.reshape(B, nb, WINDOW, SWA_KV_HEADS, HEAD_DIM)
    pad = ((0, 0), (1, 0), (0, 0), (0, 0), (0, 0))
    kk = jnp.concatenate([jnp.pad(kb, pad)[:, :-1], kb], axis=2)
    vv = jnp.concatenate([jnp.pad(vb, pad)[:, :-1], vb], axis=2)
    logits = jnp.einsum('bnqhgd,bnkhd->bnhgqk', qb, kk).astype(jnp.float32) * scale
    blk = jnp.arange(nb)[:, None, None]
    qi = jnp.arange(WINDOW)[None, :, None] + WINDOW
    kj = jnp.arange(2 * WINDOW)[None, None, :]
    diff = qi - kj
    allowed = (diff >= 0) & (diff < WINDOW) & (blk * WINDOW + kj - WINDOW >= 0)
    logits = jnp.where(allowed[None, :, None, None], logits, -jnp.inf)
    sink_col = jnp.broadcast_to(
        sinks.astype(jnp.float32).reshape(SWA_KV_HEADS, SWA_GROUP)[None, None, :, :, None, None],
        logits.shape[:-1] + (1,))
    probs = jax.nn.softmax(jnp.concatenate([logits, sink_col], axis=-1), axis=-1)[..., :-1]
    out = jnp.einsum('bnhgqk,bnkhd->bnqhgd', probs.astype(v.dtype), vv)
    return out.reshape(B, S, SWA_Q_HEADS * HEAD_DIM)


def forgetting_attention(q, k, v, log_f):
    B, S = q.shape[0], q.shape[1]
    nb = S // FOX_BLOCK
    scale = HEAD_DIM ** -0.5
    qh = jnp.transpose(q, (0, 2, 1, 3))
    kh = jnp.transpose(k, (0, 2, 1, 3))
    vh = jnp.transpose(v, (0, 2, 1, 3))
    c = jnp.cumsum(jnp.transpose(log_f, (0, 2, 1)), axis=-1)
    key_pos = jnp.arange(S)

    def block(i):
        start = i * FOX_BLOCK
        qi = lax.dynamic_slice_in_dim(qh, start, FOX_BLOCK, axis=2)
        ci = lax.dynamic_slice_in_dim(c, start, FOX_BLOCK, axis=2)
        logits = jnp.einsum('bhqd,bhkd->bhqk', qi, kh).astype(jnp.float32) * scale
        logits = logits + (ci[..., :, None] - c[..., None, :])
        qpos = start + jnp.arange(FOX_BLOCK)
        causal = key_pos[None, :] <= qpos[:, None]
        logits = jnp.where(causal[None, None], logits, -jnp.inf)
        probs = jax.nn.softmax(logits, axis=-1)
        return jnp.einsum('bhqk,bhkd->bhqd', probs.astype(vh.dtype), vh)

    out = lax.map(block, jnp.arange(nb))
    out = jnp.transpose(out, (1, 0, 3, 2, 4))
    return out.reshape(B, S, FOX_HEADS * HEAD_DIM)


def setup_inputs(seed: int = 0) -> dict:
    key = jax.random.key(seed)
    ks = jax.random.split(key, 14)
    f32 = jnp.float32
    x = jax.random.normal(ks[0], (BATCH, SEQ, D_MODEL), f32)
    positions = jnp.broadcast_to(jnp.arange(SEQ, dtype=jnp.int32)[None, :], (BATCH, SEQ))
    attn_norm = 1.0 + 0.05 * jax.random.normal(ks[1], (DEPTH, D_MODEL), f32)
    w_in = jax.random.normal(ks[2], (DEPTH, D_MODEL, D_IN), f32) * D_MODEL ** -0.5
    fox_f_bias = jax.random.uniform(ks[3], (DEPTH, FOX_HEADS), f32, 1.0, 6.0)
    swa_sinks = 0.5 * jax.random.normal(ks[4], (DEPTH, SWA_Q_HEADS), f32)
    w_branch_swa = jax.random.normal(ks[5], (DEPTH, SWA_Q_W, D_MODEL), f32) * SWA_Q_W ** -0.5
    w_branch_fox = jax.random.normal(ks[6], (DEPTH, FOX_W, D_MODEL), f32) * FOX_W ** -0.5
    w_out = jax.random.normal(ks[7], (DEPTH, D_MODEL, D_MODEL), f32) * D_MODEL ** -0.5
    mlp_norm = 1.0 + 0.05 * jax.random.normal(ks[8], (DEPTH, D_MODEL), f32)
    w_up = jax.random.normal(ks[9], (DEPTH, D_MODEL, D_FF), f32) * D_MODEL ** -0.5
    w_down = jax.random.normal(ks[10], (DEPTH, D_FF, D_MODEL), f32) * D_FF ** -0.5
    final_norm = 1.0 + 0.05 * jax.random.normal(ks[11], (D_MODEL,), f32)
    return {"x": x, "positions": positions, "attn_norm": attn_norm, "w_in": w_in,
            "fox_f_bias": fox_f_bias, "swa_sinks": swa_sinks, "w_branch_swa": w_branch_swa,
            "w_branch_fox": w_branch_fox, "w_out": w_out, "mlp_norm": mlp_norm,
            "w_up": w_up, "w_down": w_down, "final_norm": final_norm}


def reference(x, positions, attn_norm, w_in, fox_f_bias, swa_sinks, w_branch_swa,
              w_branch_fox, w_out, mlp_norm, w_up, w_down, final_norm):
    B, S = x.shape[0], x.shape[1]
    for l in range(DEPTH):
        h = rmsnorm(x, attn_norm[l])
        proj = jnp.einsum('bsd,de->bse', h, w_in[l])
        (a_q, a_k, a_v, f_q, f_k, f_v, f_logit, g_a, g_b) = jnp.split(proj, IN_SPLITS, axis=-1)
        a_q = apply_rope(a_q.reshape(B, S, SWA_Q_HEADS, HEAD_DIM), positions)
        a_k = apply_rope(a_k.reshape(B, S, SWA_KV_HEADS, HEAD_DIM), positions)
        a_v = a_v.reshape(B, S, SWA_KV_HEADS, HEAD_DIM)
        o_a = sliding_window_gqa_sinks(a_q, a_k, a_v, swa_sinks[l])
        log_f = jax.nn.log_sigmoid(f_logit.astype(jnp.float32) + fox_f_bias[l].astype(jnp.float32))
        o_b = forgetting_attention(f_q.reshape(B, S, FOX_HEADS, HEAD_DIM),
                                   f_k.reshape(B, S, FOX_HEADS, HEAD_DIM),
                                   f_v.reshape(B, S, FOX_HEADS, HEAD_DIM), log_f)
        merged = (jax.nn.sigmoid(g_a) * jnp.einsum('bse,ed->bsd', o_a, w_branch_swa[l])
                  + jax.nn.sigmoid(g_b) * jnp.einsum('bse,ed->bsd', o_b, w_branch_fox[l]))
        x = x + jnp.einsum('bsd,de->bse', merged, w_out[l])
        h = rmsnorm(x, mlp_norm[l])
        u = jax.nn.relu(jnp.einsum('bsd,df->bsf', h, w_up[l]))
        x = x + jnp.einsum('bsf,fd->bsd', u * u, w_down[l])
    return rmsnorm(x, final_norm)
```

```python
import math
from contextlib import ExitStack
import numpy as np
import ml_dtypes
import concourse.bass as bass
import concourse.mybir as mybir
from concourse.bass_utils import run_bass_kernel_spmd

F32 = mybir.dt.float32
BF16 = mybir.dt.bfloat16
I32 = mybir.dt.int32
AF = mybir.ActivationFunctionType
ALU = mybir.AluOpType

D = 2048
DFF = 8192
NH = 16
HD = 64
NEG = -30000.0
MAGIC = 12582912.0
EPS = 1e-6


class _Rec:
    def __getattr__(self, name):
        def f(*a, **kw):
            return (name, a, kw)
        return f


REC = _Rec()


def _play(e, call):
    return getattr(e, call[0])(*call[1], **call[2])


class Buf:
    __slots__ = ("name", "w", "r", "dsem", "dcnt")

    def __init__(self, name=""):
        self.name = name
        self.w = None
        self.r = {}
        self.dsem = None
        self.dcnt = 0


class Ctx:
    def __init__(self, nc, stack):
        self.nc = nc
        self.stack = stack
        self.sems = {}
        self.engs = {}
        self.free_dsems = []
        self.dvals = {}

    def sem(self, key):
        if key not in self.sems:
            self.sems[key] = self.stack.enter_context(self.nc.semaphore(key))
        return self.sems[key]

    def new_dsem(self):
        if self.free_dsems:
            return self.free_dsems.pop()
        k = "d%d" % len(self.sems)
        self.sem(k)
        self.dvals[k] = 0
        return k

    def engine(self, name, eng):
        e = Eng(self, name, eng)
        self.engs[name] = e
        return e

    def end_phase(self, bufs):
        sp = self.engs["sp"]
        toks = [(k, v) for k, v in self.dvals.items() if v]
        for e in self.engs.values():
            if e.n:
                toks.append((e.key, e.n))
        for e in self.engs.values():
            e._wait(toks)
        for b in bufs:
            if b.dsem is not None:
                self.free_dsems.append(b.dsem)
                b.dsem = None

    def emit(self):
        nc = self.nc
        E = self.engs
        with nc.Block() as block:
            @block.sync
            def _(e):
                E["sp"].replay(e)

            @block.tensor
            def _(e):
                E["pe"].replay(e)

            @block.scalar
            def _(e):
                E["act"].replay(e)

            @block.vector
            def _(e):
                E["dve"].replay(e)

            @block.gpsimd
            def _(e):
                E["pool"].replay(e)


class Eng:
    def __init__(self, ctx, name, eng):
        self.ctx = ctx
        self.name = name
        self.key = "s_" + name
        ctx.sem(self.key)
        self.n = 0
        self.seen = {}
        self.prog = []
        self.inorder = (name == "pe")
        self.nops = 0

    def _wait(self, toks):
        need = {}
        for t in toks:
            if t is None:
                continue
            k, v = t
            if v > need.get(k, 0):
                need[k] = v
        for k, v in need.items():
            if self.seen.get(k, 0) >= v:
                continue
            if k == self.key and (self.inorder or v <= self.n - 1):
                continue
            sh = self.ctx.sems[k]
            self.prog.append(lambda e, sh=sh, v=v: e.wait_ge(sh, v))
            self.seen[k] = v

    @staticmethod
    def _deps(reads, writes):
        toks = []
        for b in reads:
            toks.append(b.w)
        for b in writes:
            toks.append(b.w)
            toks.extend(b.r.items())
        return toks

    @staticmethod
    def _mark(tok, reads, writes):
        k, v = tok
        for b in reads:
            if b.r.get(k, 0) < v:
                b.r[k] = v
        for b in writes:
            b.w = tok
            b.r = {}

    def op(self, fn, reads=(), writes=()):
        self._wait(self._deps(reads, writes))
        self.n += 1
        self.nops += 1
        sh = self.ctx.sems[self.key]
        call = fn(REC)
        self.prog.append(lambda e, call=call, sh=sh: _play(e, call).then_inc(sh, 1))
        tok = (self.key, self.n)
        self._mark(tok, reads, writes)
        return tok

    def group(self, fns, reads=(), writes=()):
        self._wait(self._deps(reads, writes))
        calls = [fn(REC) for fn in fns]
        for call in calls[:-1]:
            self.prog.append(lambda e, call=call: _play(e, call))
        self.nops += len(calls)
        self.n += 1
        sh = self.ctx.sems[self.key]
        self.prog.append(lambda e, call=calls[-1], sh=sh: _play(e, call).then_inc(sh, 1))
        tok = (self.key, self.n)
        self._mark(tok, reads, writes)
        return tok

    def dma(self, pairs, reads=(), writes=(), dbuf=None):
        if dbuf is None:
            dbuf = writes[0] if writes else reads[0]
        if dbuf.dsem is None:
            dbuf.dsem = self.ctx.new_dsem()
            dbuf.dcnt = self.ctx.dvals[dbuf.dsem]
        toks = self._deps(reads, writes)
        if dbuf.dcnt:
            toks.append((dbuf.dsem, dbuf.dcnt))
        self._wait(toks)
        sh = self.ctx.sems[dbuf.dsem]
        for (o, i) in pairs:
            self.prog.append(lambda e, o=o, i=i, sh=sh: e.dma_start(out=o, in_=i).then_inc(sh, 16))
        dbuf.dcnt += 16 * len(pairs)
        self.ctx.dvals[dbuf.dsem] = dbuf.dcnt
        tok = (dbuf.dsem, dbuf.dcnt)
        self._mark(tok, reads, writes)
        return tok

    def replay(self, e):
        for f in self.prog:
            f(e)
        self.prog = []


def build(NBL, debug=False, upto=99):
    NBA = 2 * NBL
    TL = NBL * 128
    TA = NBA * 128
    NGL = NBL // 4
    NGA = NBA // 4
    nc = bass.Bass("TRN2", target_bir_lowering=False)

    def din(name, shape, dt):
        return nc.dram_tensor(name, shape, dt, kind="ExternalInput").ap()

    def scr(name, shape, dt):
        return nc.dram_tensor(name, shape, dt, kind="ExternalOutput" if debug else "Internal").ap()

    x_in = din("x", [TA, D], F32)
    pos_in = din("pos", [1, TA], I32)
    an_in = din("attn_norm", [1, D], F32)
    wall_in = din("w_all", [D, 2704], F32)
    wloc_in = din("w_loc", [D, 7168], F32)
    fb_in = din("fbias", [1, NH], F32)
    sk_in = din("sinks", [1, NH], F32)
    wa_in = din("wa", [1024, D], F32)
    wb_in = din("wb", [1024, D], F32)
    wo_in = din("wout", [D, D], F32)
    mn_in = din("mlp_norm", [1, D], F32)
    wu_in = din("w_up", [D, DFF], F32)
    wd_in = din("w_down", [DFF, D], F32)
    fn_in = din("final_norm", [1, D], F32)
    id_in = din("ident", [128, 128], BF16)
    cm_in = din("cmat", [128, 4, 128], F32)
    tm_in = din("trimask", [128, 128], BF16)
    sm_in = din("swamask", [128, 2, 2, 128], BF16)
    fl_in = din("flags", [128, 6], F32)
    rc_in = din("ropec", [128, 2], F32)
    fm_in = din("foxmask", [128, 2, 128], BF16)
    m4_in = din("swamask4", [128, 5, 512], BF16)
    em_in = din("emat", [65, 64], F32)
    y_out = nc.dram_tensor("y", [TL, D], F32, kind="ExternalOutput").ap()

    HT = scr("HT", [16, 128, TA], BF16)
    KTF = scr("KTF", [8, 2, 65, TA], BF16)
    KTA = scr("KTA", [2, 128, TA], BF16)
    VF = scr("VF", [NBA, 128, NH * 65], BF16)
    VA = scr("VA", [NBA, 128, 2 * 65], BF16)
    QTF = scr("QTF", [8, 2, 65, TL], BF16)
    WUB = scr("WUB", [16, 128, 16, 512], BF16)
    WDB = scr("WDB", [8, 4, 128, 16, 256], BF16)
    QTA = scr("QTA", [8, 128, TL], BF16)
    GAT = scr("GAT", [16, 128, TL], BF16)
    GBT = scr("GBT", [16, 128, TL], BF16)
    NCS = scr("NCS", [128, NBA * NH], F32)
    NCT = scr("NCT", [128, NBL * NH], F32)
    OBT = scr("OBT", [8, 128, TL], BF16)
    OAT = scr("OAT", [8, 128, TL], BF16)
    MT = scr("MT", [16, 128, TL], BF16)
    X1 = scr("X1", [TL, D], F32)
    H2T = scr("H2T", [16, 128, TL], BF16)

    with ExitStack() as gst:
        ctx = Ctx(nc, gst)
        sp = ctx.engine("sp", nc.sync)
        pe = ctx.engine("pe", nc.tensor)
        act = ctx.engine("act", nc.scalar)
        dve = ctx.engine("dve", nc.vector)
        pool = ctx.engine("pool", nc.gpsimd)

        class Phase:
            _n = [0]

            def __init__(self):
                self.st = ExitStack()
                self.bufs = []
                Phase._n[0] += 1
                self.pfx = "p%d_" % Phase._n[0]

            def sb(self, name, shape, dt, nb=1):
                ts = [self.st.enter_context(nc.sbuf_tensor(self.pfx + "%s_%d" % (name, i), shape, dt)) for i in range(nb)]
                bs = [Buf("%s%d" % (name, i)) for i in range(nb)]
                self.bufs.extend(bs)
                return (ts, bs) if nb > 1 else (ts[0], bs[0])

            def ps(self, name, shape, dt, nb=1):
                ts = [self.st.enter_context(nc.psum_tensor(self.pfx + "%s_%d" % (name, i), shape, dt)) for i in range(nb)]
                bs = [Buf("%s%d" % (name, i)) for i in range(nb)]
                self.bufs.extend(bs)
                return (ts, bs) if nb > 1 else (ts[0], bs[0])

            def close(self):
                ctx.end_phase(self.bufs)
                ctx.emit()
                self.st.close()

        def rmsnorm_to_bf16(P, xt, xb, gt, gb, ht, hb, ss, ssb, junk, jb, epsc, epb):
            dve.op(lambda e: e.scalar_tensor_tensor(out=junk[:], in0=xt[:], scalar=1.0, in1=xt[:], op0=ALU.mult,
                                                    op1=ALU.mult, accum_out=ss[:]),
                   reads=[xb], writes=[jb, ssb])
            act.op(lambda e: e.activation(out=ss[:], in_=ss[:], func=AF.Sqrt, scale=1.0 / D, bias=epsc[:]),
                   reads=[ssb, epb], writes=[ssb])
            dve.op(lambda e: e.reciprocal(out=ss[:], in_=ss[:]), reads=[ssb], writes=[ssb])
            dve.op(lambda e: e.scalar_tensor_tensor(out=ht[:], in0=xt[:], scalar=ss[:], in1=gt[:], op0=ALU.mult,
                                                    op1=ALU.mult),
                   reads=[xb, ssb, gb], writes=[hb])

        def transpose16(ht, hb, idt, idb, pT, pTb, dst, dstb, blk, cnt):
            for hf in range(2):
                k = cnt[0] % len(pT)
                cnt[0] += 1
                pe.group([(lambda e, c=c, k=k: e.transpose(out=pT[k][:, c % 8, :], in_=ht[:, c * 128:(c + 1) * 128],
                                                            identity=idt[:])) for c in range(hf * 8, hf * 8 + 8)],
                         reads=[hb, idb], writes=[pTb[k]])
                act.op(lambda e, k=k, hf=hf: e.activation(out=dst[:, hf * 8:hf * 8 + 8, blk * 128:(blk + 1) * 128],
                                                          in_=pT[k][:], func=AF.Copy),
                       reads=[pTb[k]], writes=[dstb])

        if upto >= 1:
            P = Phase()
            xt, xb = P.sb("xt", [128, D], F32, 3)
            gt, gb = P.sb("gt", [128, D], F32)
            ht, hb = P.sb("ht", [128, D], BF16, 2)
            junk, jb = P.sb("junk", [128, D], BF16)
            ss, ssb = P.sb("ss", [128, 1], F32, 3)
            epsc, epb = P.sb("epsc", [128, 1], F32)
            idt, idb = P.sb("idt", [128, 128], BF16)
            hTg, hTb = P.sb("hTg", [128, 16, 512], BF16, 2)
            pT, pTb = P.ps("pT", [128, 8, 128], BF16, 4)
            sp.dma([(gt[:], an_in.broadcast_to([128, D]))], writes=[gb])
            sp.dma([(idt[:], id_in)], writes=[idb])
            dve.op(lambda e: e.memset(epsc[:], EPS), writes=[epb])
            cnt = [0]
            def part1(s):
                i3 = s % 3
                sp.dma([(xt[i3][:], x_in[s * 128:(s + 1) * 128, :])], writes=[xb[i3]])
                dve.op(lambda e: e.scalar_tensor_tensor(out=junk[:], in0=xt[i3][:], scalar=1.0, in1=xt[i3][:],
                                                        op0=ALU.mult, op1=ALU.mult, accum_out=ss[i3][:]),
                       reads=[xb[i3]], writes=[jb, ssb[i3]])
                act.op(lambda e: e.activation(out=ss[i3][:], in_=ss[i3][:], func=AF.Sqrt, scale=1.0 / D, bias=epsc[:]),
                       reads=[ssb[i3], epb], writes=[ssb[i3]])

            def part2(s):
                i3 = s % 3
                i = s % 2
                dve.op(lambda e: e.reciprocal(out=ss[i3][:], in_=ss[i3][:]), reads=[ssb[i3]], writes=[ssb[i3]])
                dve.op(lambda e: e.scalar_tensor_tensor(out=ht[i][:], in0=xt[i3][:], scalar=ss[i3][:], in1=gt[:],
                                                        op0=ALU.mult, op1=ALU.mult),
                       reads=[xb[i3], ssb[i3], gb], writes=[hb[i]])

            part1(0)
            for s in range(NBA):
                i = s % 2
                g = s // 4
                if s + 1 < NBA:
                    part1(s + 1)
                part2(s)
                transpose16(ht[i], hb[i], idt, idb, pT, pTb, hTg[g % 2], hTb[g % 2], s % 4, cnt)
                if s % 4 == 3:
                    pool.dma([(HT[:, :, g * 512:(g + 1) * 512].rearrange("c p t -> p c t"), hTg[g % 2][:])],
                             reads=[hTb[g % 2]])
            P.close()

        def rope_tables(P, T, g):
            (posi, posib, u, ub, t1, t1b, kk, kkb, dd, ddb, cosT, cosb, sinT, sinb, rc, rcb) = T
            sp.dma([(posi[:], pos_in[:, g * 512:(g + 1) * 512].broadcast_to([128, 512]))], writes=[posib])
            dve.op(lambda e: e.tensor_copy(out=u[:], in_=posi[:]), reads=[posib], writes=[ub])
            dve.op(lambda e: e.tensor_scalar(out=u[:], in0=u[:], scalar1=rc[:, 0:1], scalar2=None, op0=ALU.mult),
                   reads=[ub, rcb], writes=[ub])
            dve.op(lambda e: e.tensor_scalar(out=t1[:], in0=u[:], scalar1=MAGIC, scalar2=None, op0=ALU.add),
                   reads=[ub], writes=[t1b])
            dve.op(lambda e: e.tensor_scalar(out=kk[:], in0=t1[:], scalar1=MAGIC, scalar2=None, op0=ALU.subtract),
                   reads=[t1b], writes=[kkb])
            dve.op(lambda e: e.tensor_tensor(out=dd[:], in0=u[:], in1=kk[:], op=ALU.subtract),
                   reads=[ub, kkb], writes=[ddb])
            act.op(lambda e: e.activation(out=sinT[:], in_=dd[:], func=AF.Sin, scale=rc[:, 1:2]),
                   reads=[ddb, rcb], writes=[sinb])
            dve.op(lambda e: e.tensor_scalar(out=t1[:], in0=u[:], scalar1=0.25, scalar2=MAGIC, op0=ALU.add,
                                             op1=ALU.add), reads=[ub], writes=[t1b])
            dve.op(lambda e: e.tensor_scalar(out=kk[:], in0=t1[:], scalar1=MAGIC, scalar2=None, op0=ALU.subtract),
                   reads=[t1b], writes=[kkb])
            dve.op(lambda e: e.scalar_tensor_tensor(out=dd[:], in0=u[:], scalar=0.25, in1=kk[:], op0=ALU.add,
                                                    op1=ALU.subtract), reads=[ub, kkb], writes=[ddb])
            act.op(lambda e: e.activation(out=cosT[:], in_=dd[:], func=AF.Sin, scale=2.0 * math.pi),
                   reads=[ddb], writes=[cosb])

        def rope_alloc(P):
            posi, posib = P.sb("posi", [128, 512], I32)
            u, ub = P.sb("ru", [128, 512], F32)
            t1, t1b = P.sb("rt1", [128, 512], F32)
            kk, kkb = t1, t1b
            dd, ddb = P.sb("rdd", [128, 512], F32)
            cosT, cosb = P.sb("cosT", [128, 512], F32)
            sinT, sinb = P.sb("sinT", [128, 512], F32)
            rc, rcb = P.sb("rc", [128, 2], F32)
            sp.dma([(rc[:], rc_in)], writes=[rcb])
            return (posi, posib, u, ub, t1, t1b, kk, kkb, dd, ddb, cosT, cosb, sinT, sinb, rc, rcb)

        def load_w_cast(dst, dstb, src_ap, ncol, c0=0, nchunk=16):
            pairs = []
            step = max(1, 4 // max(1, ncol // 512))
            for c in range(0, nchunk, step):
                pairs.append((dst[:, c0 + c:c0 + c + step, 0:ncol],
                              src_ap[c * 128:(c + step) * 128, :].rearrange("(c p) n -> p c n", p=128)))
            pool.dma(pairs, writes=[dstb])

        def ft_tile(ps, psb, w, wb_, wcol, hTg, hTgb):
            pe.group([(lambda e, c=c: e.matmul(ps[:], lhsT=w[:, c, wcol:wcol + 128], rhs=hTg[:, c, :],
                                               start=(c == 0), stop=(c == 15))) for c in range(16)],
                     reads=[wb_, hTgb], writes=[psb])

        def rope_apply(ps_t, psb_t, ps_s, psb_s, T, tmp1, tmp1b, tmp2, tmp2b, dst_ap, dstb):
            cosT, cosb, sinT, sinb = T[10], T[11], T[12], T[13]
            dve.op(lambda e: e.tensor_tensor(out=tmp1[:], in0=ps_t[:], in1=cosT[:], op=ALU.mult),
                   reads=[psb_t, cosb], writes=[tmp1b])
            dve.op(lambda e: e.tensor_tensor(out=tmp2[:], in0=ps_s[:], in1=sinT[:], op=ALU.mult),
                   reads=[psb_s, sinb], writes=[tmp2b])
            pool.op(lambda e: e.tensor_tensor(out=dst_ap, in0=tmp1[:], in1=tmp2[:], op=ALU.add),
                    reads=[tmp1b, tmp2b], writes=[dstb])

        if upto >= 2:
            P = Phase()
            w, wb_ = P.sb("wall", [128, 16, 2704], BF16)
            hTg, hTgb = P.sb("hTg", [128, 16, 512], BF16, 2)
            T = rope_alloc(P)
            tmp1, tmp1b = P.sb("tmp1", [128, 512], F32)
            tmp2, tmp2b = P.sb("tmp2", [128, 512], F32)
            kst, kstb = P.sb("kst", [128, 8, 512], BF16, 1)
            kst, kstb = [kst, kst], [kstb, kstb]
            kast, kastb = P.sb("kast", [128, 2, 512], BF16, 2)
            vst, vstb = P.sb("vst", [128, 4, NH * 65], BF16, 1)
            vst, vstb = [vst, vst], [vstb, vstb]
            vast, vastb = P.sb("vast", [128, 4, 2 * 65], BF16, 2)
            fbt, fbb = P.sb("fbt", [128, NH], F32)
            nl, nlb = P.sb("nl", [128, NBA * NH], F32)
            zz, zzb = P.sb("zz", [128, NH], F32)
            cm, cmb = P.sb("cm", [128, 4, 128], F32)
            fl, flb = P.sb("fl", [128, 6], F32)
            wi, wib = P.sb("wi", [128, NBA * NH], F32)
            tot, totb = P.sb("tot", [128, NBA * NH], F32)
            hal, halb = P.sb("hal", [128, NBL * NH], F32)
            run, runb = P.sb("run", [128, NH], F32)
            ca, cab = P.sb("ca", [128, NH], F32)
            cb2, cb2b = P.sb("cb2", [128, NH], F32)
            ncs, ncsb = P.sb("ncs", [128, NBA * NH], F32)
            nct, nctb = P.sb("nct", [128, NBL * NH], F32)
            ps, psb = P.ps("ps", [128, 512], F32, 8)
            for i in range(2):
                dve.op(lambda e, i=i: e.memset(vst[i][:], 1.0), writes=[vstb[i]])
                dve.op(lambda e, i=i: e.memset(vast[i][:], 1.0), writes=[vastb[i]])
            onesr, onesrb = P.sb("onesr", [16, TA], BF16)
            shT, shTb = P.sb("shT", [16, TL], BF16)
            vsh, vshb = P.sb("vsh", [128, NH], F32, 2)
            dve.op(lambda e: e.memset(onesr[:], 1.0), writes=[onesrb])
            sp.dma([(KTF.rearrange("j h d t -> (j h) d t")[:, 64, :], onesr[:])], reads=[onesrb])
            sp.dma([(fbt[:], fb_in.broadcast_to([128, NH]))], writes=[fbb])
            sp.dma([(cm[:], cm_in)], writes=[cmb])
            sp.dma([(fl[:], fl_in)], writes=[flb])
            for c0 in range(0, 2704, 512):
                n = min(512, 2704 - c0)
                pairs = []
                for c in range(0, 16, 4):
                    pairs.append((w[:, c:c + 4, c0:c0 + n],
                                  wall_in[c * 128:(c + 4) * 128, c0:c0 + n].rearrange("(c p) n -> p c n", p=128)))
                pool.dma(pairs, writes=[wb_])
            pi = [0]

            def nps():
                k = pi[0] % 8
                pi[0] += 1
                return k

            for g in range(NGA):
                hb_i = g % 2
                sp.dma([(hTg[hb_i][:], HT[:, :, g * 512:(g + 1) * 512].rearrange("c p t -> p c t"))],
                       writes=[hTgb[hb_i]])
                rope_tables(P, T, g)
                for j in range(8):
                    k = nps()
                    ft_tile(ps[k], psb[k], w, wb_, j * 128, hTg[hb_i], hTgb[hb_i])
                    act.op(lambda e, k=k, j=j: e.activation(out=kst[hb_i][:, j, :], in_=ps[k][:], func=AF.Copy),
                           reads=[psb[k]], writes=[kstb[hb_i]])
                pool.dma([(KTF[:, hh, 0:64, g * 512:(g + 1) * 512].rearrange("j d t -> d j t"),
                           kst[hb_i][hh * 64:(hh + 1) * 64, :, :]) for hh in range(2)], reads=[kstb[hb_i]])
                for kv in range(2):
                    k1 = nps()
                    ft_tile(ps[k1], psb[k1], w, wb_, 1024 + kv * 256, hTg[hb_i], hTgb[hb_i])
                    k2 = nps()
                    ft_tile(ps[k2], psb[k2], w, wb_, 1024 + kv * 256 + 128, hTg[hb_i], hTgb[hb_i])
                    rope_apply(ps[k1], psb[k1], ps[k2], psb[k2], T, tmp1, tmp1b, tmp2, tmp2b,
                               kast[hb_i][:, kv, :], kastb[hb_i])
                pool.dma([(KTA[:, :, g * 512:(g + 1) * 512].rearrange("c p t -> p c t"), kast[hb_i][:])],
                         reads=[kastb[hb_i]])
                for bl in range(4):
                    s = g * 4 + bl
                    for hf in range(2):
                        k = nps()
                        pe.group([(lambda e, c=c, k=k: e.matmul(ps[k][:], lhsT=hTg[hb_i][:, c, bl * 128:(bl + 1) * 128],
                                                                rhs=w[:, c, 1536 + hf * 512:1536 + (hf + 1) * 512],
                                                                start=(c == 0), stop=(c == 15))) for c in range(16)],
                                 reads=[wb_, hTgb[hb_i]], writes=[psb[k]])
                        dve.op(lambda e, k=k, hf=hf: e.tensor_copy(
                            out=vst[hb_i][:, bl, hf * 8 * 65:(hf + 1) * 8 * 65].rearrange("p (h d) -> p h d", d=65)[:, :, 0:64],
                            in_=ps[k][:].rearrange("p (h d) -> p h d", d=64)), reads=[psb[k]], writes=[vstb[hb_i]])
                    k = nps()
                    pe.group([(lambda e, c=c, k=k: e.matmul(ps[k][:, 0:144], lhsT=hTg[hb_i][:, c, bl * 128:(bl + 1) * 128],
                                                            rhs=w[:, c, 2560:2704],
                                                            start=(c == 0), stop=(c == 15))) for c in range(16)],
                             reads=[wb_, hTgb[hb_i]], writes=[psb[k]])
                    dve.op(lambda e, k=k: e.tensor_copy(
                        out=vast[hb_i][:, bl, :].rearrange("p (h d) -> p h d", d=65)[:, :, 0:64],
                        in_=ps[k][:, 0:128].rearrange("p (h d) -> p h d", d=64)), reads=[psb[k]], writes=[vastb[hb_i]])
                    dve.op(lambda e, k=k: e.tensor_tensor(out=zz[:], in0=ps[k][:, 128:144], in1=fbt[:], op=ALU.add),
                           reads=[psb[k], fbb], writes=[zzb])
                    act.op(lambda e: e.activation(out=zz[:], in_=zz[:], func=AF.Exp, scale=-1.0),
                           reads=[zzb], writes=[zzb])
                    act.op(lambda e, s=s: e.activation(out=nl[:, s * NH:(s + 1) * NH], in_=zz[:], func=AF.Ln, bias=1.0),
                           reads=[zzb], writes=[nlb])
                pool.dma([(VF[g * 4:(g + 1) * 4].rearrange("s p n -> p s n"), vst[hb_i][:])], reads=[vstb[hb_i]])
                pool.dma([(VA[g * 4:(g + 1) * 4].rearrange("s p n -> p s n"), vast[hb_i][:])], reads=[vastb[hb_i]])
            for m, (dst, dstb, nsl) in enumerate([(wi, wib, NBA), (tot, totb, NBA), (hal, halb, NBL)]):
                for c0 in range(0, nsl * NH, 512):
                    n = min(512, nsl * NH - c0)
                    k = nps()
                    pe.group([lambda e, k=k, m=m, c0=c0, n=n: e.matmul(ps[k][:, 0:n], lhsT=cm[:, m, :],
                                                                      rhs=nl[:, c0:c0 + n], start=True, stop=True)],
                             reads=[cmb, nlb], writes=[psb[k]])
                    dve.op(lambda e, k=k, dst=dst, c0=c0, n=n: e.tensor_copy(out=dst[:, c0:c0 + n], in_=ps[k][:, 0:n]),
                           reads=[psb[k]], writes=[dstb])
            dve.op(lambda e: e.memset(run[:], 0.0), writes=[runb])
            OW = NBL * NH
            for t in range(NBL):
                p = t % 2
                so = slice(t * NH, (t + 1) * NH)
                st_ = slice(OW + t * NH, OW + (t + 1) * NH)
                dve.op(lambda e, p=p, st_=st_: e.scalar_tensor_tensor(out=ca[:], in0=tot[:, st_], scalar=fl[:, p:p + 1],
                                                                      in1=run[:], op0=ALU.mult, op1=ALU.add),
                       reads=[totb, flb, runb], writes=[cab])
                dve.op(lambda e, p=p, so=so: e.scalar_tensor_tensor(out=cb2[:], in0=tot[:, so], scalar=fl[:, 2 + p:3 + p],
                                                                    in1=run[:], op0=ALU.mult, op1=ALU.add),
                       reads=[totb, flb, runb], writes=[cb2b])
                dve.op(lambda e, so=so: e.tensor_tensor(out=ncs[:, so], in0=wi[:, so], in1=ca[:], op=ALU.add),
                       reads=[wib, cab], writes=[ncsb])
                dve.op(lambda e, so=so: e.scalar_tensor_tensor(out=nct[:, so], in0=hal[:, so], scalar=-1.0, in1=ca[:],
                                                               op0=ALU.mult, op1=ALU.subtract),
                       reads=[halb, cab], writes=[nctb])
                dve.op(lambda e, st_=st_: e.tensor_tensor(out=ncs[:, st_], in0=wi[:, st_], in1=cb2[:], op=ALU.add),
                       reads=[wib, cb2b], writes=[ncsb])
                dve.op(lambda e, so=so: e.tensor_tensor(out=run[:], in0=run[:], in1=tot[:, so], op=ALU.add),
                       reads=[runb, totb], writes=[runb])
                dve.op(lambda e, st_=st_: e.tensor_tensor(out=run[:], in0=run[:], in1=tot[:, st_], op=ALU.add),
                       reads=[runb, totb], writes=[runb])
            for t in range(NBL):
                rf = min(4 * (t // 4) + 2, NBL - 1)
                so = slice(t * NH, (t + 1) * NH)
                v_ = t % 2
                k = nps()
                dve.op(lambda e, so=so, rf=rf, v_=v_: e.tensor_tensor(out=vsh[v_][:], in0=ncs[:, so],
                                                                      in1=nct[:, rf * NH:(rf + 1) * NH], op=ALU.add),
                       reads=[ncsb, nctb], writes=[vshb[v_]])
                pe.group([lambda e, k=k, v_=v_: e.transpose(out=ps[k][0:16, 0:128], in_=vsh[v_][:], identity=cm[:, 3, :])],
                         reads=[vshb[v_], cmb], writes=[psb[k]])
                dve.op(lambda e, k=k, t=t: e.tensor_scalar(out=shT[:, t * 128:(t + 1) * 128], in0=ps[k][0:16, 0:128],
                                                           scalar1=-8.0, scalar2=None, op0=ALU.mult),
                       reads=[psb[k]], writes=[shTb])
            sp.dma([(QTF.rearrange("j h d t -> (j h) d t")[:, 64, :], shT[:])], reads=[shTb])
            sp.dma([(NCS, ncs[:])], reads=[ncsb])
            sp.dma([(NCT, nct[:])], reads=[nctb])
            P.close()

        if upto >= 3:
            P = Phase()
            w, wb_ = P.sb("wloc", [128, 16, 1024], BF16, 2)
            hTg, hTgb = P.sb("hTg", [128, 16, 512], BF16, 2)
            T = rope_alloc(P)
            tmp1, tmp1b = P.sb("tmp1", [128, 512], F32)
            tmp2, tmp2b = P.sb("tmp2", [128, 512], F32)
            stg, stgb = P.sb("stg", [128, 8, 512], BF16, 2)
            ps, psb = P.ps("ps", [128, 512], F32, 8)
            pi = [0]
            gi = 0
            dsts = [(QTF, 0, "copy"), (QTA, 0, "rope"), (QTA, 4, "rope"), (GAT, 0, "sig"), (GAT, 8, "sig"),
                    (GBT, 0, "sig"), (GBT, 8, "sig")]
            load_w_cast(w[0], wb_[0], wloc_in[:, 0:1024], 1024)
            for sl in range(7):
                wi_ = sl % 2
                if sl + 1 < 7:
                    load_w_cast(w[1 - wi_], wb_[1 - wi_], wloc_in[:, (sl + 1) * 1024:(sl + 2) * 1024], 1024)
                dst, t0, kind = dsts[sl]
                nt = 4 if kind == "rope" else 8
                for g in range(NGL):
                    hb_i = gi % 2
                    gi += 1
                    sp.dma([(hTg[hb_i][:], HT[:, :, g * 512:(g + 1) * 512].rearrange("c p t -> p c t"))],
                           writes=[hTgb[hb_i]])
                    if kind == "rope":
                        rope_tables(P, T, g)
                    for j in range(nt):
                        if kind == "rope":
                            k1 = pi[0] % 8
                            k2 = (pi[0] + 1) % 8
                            pi[0] += 2
                            ft_tile(ps[k1], psb[k1], w[wi_], wb_[wi_], j * 256, hTg[hb_i], hTgb[hb_i])
                            ft_tile(ps[k2], psb[k2], w[wi_], wb_[wi_], j * 256 + 128, hTg[hb_i], hTgb[hb_i])
                            rope_apply(ps[k1], psb[k1], ps[k2], psb[k2], T, tmp1, tmp1b, tmp2, tmp2b,
                                       stg[hb_i][:, j, :], stgb[hb_i])
                        else:
                            k = pi[0] % 8
                            pi[0] += 1
                            ft_tile(ps[k], psb[k], w[wi_], wb_[wi_], j * 128, hTg[hb_i], hTgb[hb_i])
                            fnc = AF.Copy if kind == "copy" else AF.Sigmoid
                            act.op(lambda e, k=k, j=j, fnc=fnc: e.activation(out=stg[hb_i][:, j, :], in_=ps[k][:], func=fnc),
                                   reads=[psb[k]], writes=[stgb[hb_i]])
                    if kind == "copy":
                        pool.dma([(QTF[:, hh, 0:64, g * 512:(g + 1) * 512].rearrange("j d t -> d j t"),
                                   stg[hb_i][hh * 64:(hh + 1) * 64, :, :]) for hh in range(2)], reads=[stgb[hb_i]])
                    else:
                        pool.dma([(dst[t0:t0 + nt, :, g * 512:(g + 1) * 512].rearrange("c p t -> p c t"),
                                   stg[hb_i][:, 0:nt, :])], reads=[stgb[hb_i]])
            P.close()

        def run_pipeline(jobs, LOOK, front, back, tail_a, tail_b, DEFER=2):
            pend = []
            nj = len(jobs)
            n = 0
            while n < nj + LOOK or pend:
                if n < nj:
                    front(jobs[n], n)
                m = n - LOOK
                if 0 <= m < nj:
                    back(jobs[m], m)
                    if jobs[m]["last"]:
                        tail_a(jobs[m])
                        pend.append((n + DEFER, jobs[m]))
                while pend and (pend[0][0] <= n or n >= nj + LOOK):
                    tail_b(pend.pop(0)[1])
                n += 1

        if upto >= 4:
            P = Phase()
            kt, ktb = P.sb("kt", [65, TA], BF16, 2)
            qt, qtb = P.sb("qt", [65, TL], BF16, 2)
            vv, vvb = P.sb("vv", [128, NBA, 130], BF16, 2)
            ncs, ncsb = P.sb("ncs", [128, NBA * NH], F32)
            nct, nctb = P.sb("nct", [128, NBL * NH], F32)
            bia, biab = P.sb("bia", [128, NBA], F32, 2)
            tmk, tmkb = P.sb("tmk", [128, 128], BF16)
            fmk, fmkb = P.sb("fmk", [128, 2, 128], BF16)
            idt, idb = P.sb("idt", [128, 128], BF16)
            em, emb = P.sb("em", [65, 64], F32)
            pt, ptb = P.sb("pt", [128, 512], BF16, 4)
            osb, osbb = P.sb("osb", [65, 512], F32, 2)
            rinv, rinvb = P.sb("rinv", [64, 512], F32, 2)
            obT, obTb = P.sb("obT", [64, TL], BF16, 2)
            wc, wcb = P.sb("wc", [128, 16, 512], BF16, 2)
            psS, psSb = P.ps("psS", [128, 512], F32, 4)
            psO, psOb = P.ps("psO", [128, 512], F32, 2)
            psL, psLb = P.ps("psL", [128, 512], F32, 1)
            sp.dma([(ncs[:], NCS)], writes=[ncsb])
            sp.dma([(nct[:], NCT)], writes=[nctb])
            sp.dma([(tmk[:], tm_in)], writes=[tmkb])
            sp.dma([(fmk[:], fm_in)], writes=[fmkb])
            sp.dma([(idt[:], id_in)], writes=[idb])
            sp.dma([(em[:], em_in)], writes=[emb])
            chunks = [("u", sl) for sl in range(16)] + [("d", ch, hf) for ch in range(8) for hf in range(4)]
            cci = [0]

            def cast_chunk():
                if cci[0] >= len(chunks):
                    return
                cks = chunks[cci[0]]
                i = cci[0] % 2
                cci[0] += 1
                if cks[0] == "u":
                    sl = cks[1]
                    pool.dma([(wc[i][:, c:c + 4, :],
                               wu_in[c * 128:(c + 4) * 128, sl * 512:(sl + 1) * 512].rearrange("(c p) n -> p c n", p=128))
                              for c in range(0, 16, 4)], writes=[wcb[i]])
                    pool.dma([(WUB[sl], wc[i][:])], reads=[wcb[i]])
                else:
                    ch, hf = cks[1], cks[2]
                    r0 = hf * 16 * 128
                    pool.dma([(wc[i][:, c:c + 8, 0:256],
                               wd_in[r0 + c * 128:r0 + (c + 8) * 128, ch * 256:(ch + 1) * 256].rearrange("(c p) n -> p c n", p=128))
                              for c in range(0, 16, 8)], writes=[wcb[i]])
                    pool.dma([(WDB[ch, hf], wc[i][:, :, 0:256])], reads=[wcb[i]])

            NT = (NBL + 3) // 4
            jobs = []
            for h in range(NH):
                for T_ in range(NT):
                    t0 = 4 * T_
                    nq = min(4, NBL - t0)
                    tiles = [(s_, 0, None) for s_ in list(range(t0)) + list(range(NBL, NBL + t0))]
                    for d_ in range(nq):
                        tiles.append((t0 + d_, d_, tmk[:]))
                        tiles.append((NBL + t0 + d_, d_, fmk[:, (t0 + d_) % 2, :]))
                    g_ = h * NT + T_
                    for i_, (s_, d_, mk) in enumerate(tiles):
                        jobs.append(dict(h=h, T=T_, t0=t0, nq=nq, s=s_, d=d_, mk=mk, g=g_, first=(i_ == 0),
                                         last=(i_ == len(tiles) - 1), hfirst=(T_ == 0 and i_ == 0)))

            def loads(h):
                j, hh = h // 2, h % 2
                sp.dma([(kt[h % 2][:], KTF[j, hh])], writes=[ktb[h % 2]])
                sp.dma([(qt[h % 2][:], QTF[j, hh])], writes=[qtb[h % 2]])
                if hh == 0:
                    sp.dma([(vv[j % 2][:], VF[:, :, j * 130:(j + 1) * 130].rearrange("s p n -> p s n"))],
                           writes=[vvb[j % 2]])

            loads(0)

            def front(jb, n):
                h, t0, nq, s_, d_, mk, g_ = jb["h"], jb["t0"], jb["nq"], jb["s"], jb["d"], jb["mk"], jb["g"]
                hb_ = h % 2
                b_ = g_ % 2
                if jb["hfirst"] and h + 1 < NH:
                    loads(h + 1)
                if jb["first"]:
                    if g_ % 2 == 0:
                        cast_chunk()
                    rf = min(t0 + 2, NBL - 1)
                    dve.op(lambda e: e.tensor_scalar(
                        out=bia[b_][:], in0=ncs[:].rearrange("p (s h) -> p s h", h=NH)[:, :, h],
                        scalar1=nct[:, rf * NH + h:rf * NH + h + 1], scalar2=None, op0=ALU.add),
                        reads=[ncsb, nctb], writes=[biab[b_]])
                k = n % 4
                q0 = (t0 + d_) * 128
                N = (nq - d_) * 128
                fns = [lambda e: e.matmul(psS[k][:, 0:N], lhsT=kt[hb_][0:65, s_ * 128:(s_ + 1) * 128],
                                          rhs=qt[hb_][0:65, q0:q0 + N], start=True, stop=(mk is None))]
                rd = [ktb[hb_], qtb[hb_]]
                if mk is not None:
                    fns.append(lambda e: e.matmul(psS[k][:, 0:128], lhsT=idt[:], rhs=mk, start=False, stop=True))
                    rd += [idb, tmkb, fmkb]
                pe.group(fns, reads=rd, writes=[psSb[k]])
                act.op(lambda e: e.activation(out=pt[k][:, 0:N], in_=psS[k][:, 0:N], func=AF.Exp, scale=0.125,
                                              bias=bia[b_][:, s_:s_ + 1]), reads=[psSb[k], biab[b_]], writes=[ptb[k]])

            def back(jb, m):
                h, nq, s_, d_, g_ = jb["h"], jb["nq"], jb["s"], jb["d"], jb["g"]
                k = m % 4
                o_ = g_ % 2
                jb_ = (h // 2) % 2
                hh = h % 2
                N = (nq - d_) * 128
                pe.group([lambda e: e.matmul(psO[o_][0:65, d_ * 128:d_ * 128 + N],
                                             lhsT=vv[jb_][:, s_, hh * 65:(hh + 1) * 65], rhs=pt[k][:, 0:N],
                                             start=jb["first"], stop=jb["last"])],
                         reads=[ptb[k], vvb[jb_]], writes=[psOb[o_]])

            def tail_a(jb):
                o_ = jb["g"] % 2
                NQ = jb["nq"] * 128
                dve.op(lambda e: e.tensor_copy(out=osb[o_][:, 0:NQ], in_=psO[o_][0:65, 0:NQ]),
                       reads=[psOb[o_]], writes=[osbb[o_]])

            def tail_b(jb):
                h, t0, g_ = jb["h"], jb["t0"], jb["g"]
                o_ = g_ % 2
                hb_ = h % 2
                NQ = jb["nq"] * 128
                pe.group([lambda e: e.matmul(psL[0:64, 0:NQ], lhsT=em[:], rhs=osb[o_][:, 0:NQ], start=True, stop=True)],
                         reads=[emb, osbb[o_]], writes=[psLb])
                dve.op(lambda e: e.reciprocal(out=rinv[o_][:, 0:NQ], in_=psL[0:64, 0:NQ]),
                       reads=[psLb], writes=[rinvb[o_]])
                dve.op(lambda e: e.tensor_tensor(out=obT[hb_][:, t0 * 128:t0 * 128 + NQ], in0=osb[o_][0:64, 0:NQ],
                                                 in1=rinv[o_][:, 0:NQ], op=ALU.mult),
                       reads=[osbb[o_], rinvb[o_]], writes=[obTb[hb_]])
                if jb["T"] == NT - 1:
                    j, hh = h // 2, h % 2
                    pool.dma([(OBT[j, hh * 64:(hh + 1) * 64, :], obT[hb_][:])], reads=[obTb[hb_]])

            run_pipeline(jobs, 3, front, back, tail_a, tail_b)
            while cci[0] < len(chunks):
                cast_chunk()
            P.close()

        if upto >= 5:
            P = Phase()
            kt, ktb = P.sb("kt", [128, 2, TA], BF16)
            qt, qtb = P.sb("qt", [128, 4, TL], BF16)
            vv, vvb = P.sb("vv", [128, NBA, 130], BF16)
            m4, m4b = P.sb("m4", [128, 5, 512], BF16)
            idt, idb = P.sb("idt", [128, 128], BF16)
            em, emb = P.sb("em", [65, 64], F32)
            skt, skb = P.sb("skt", [64, NH], F32)
            skr, skrb = P.sb("skr", [64, NH, 128], F32)
            pt, ptb = P.sb("pt", [128, 512], BF16, 3)
            osb, osbb = P.sb("osb", [65, 512], F32, 2)
            rinv, rinvb = P.sb("rinv", [64, 512], F32, 2)
            oaT, oaTb = P.sb("oaT", [64, 4, TL], BF16, 2)
            psS, psSb = P.ps("psS", [128, 512], F32, 3)
            psO, psOb = P.ps("psO", [128, 512], F32, 2)
            psL, psLb = P.ps("psL", [128, 512], F32, 1)
            sp.dma([(kt[:], KTA.rearrange("c p t -> p c t"))], writes=[ktb])
            sp.dma([(vv[:], VA.rearrange("s p n -> p s n"))], writes=[vvb])
            sp.dma([(m4[:], m4_in)], writes=[m4b])
            sp.dma([(idt[:], id_in)], writes=[idb])
            sp.dma([(em[:], em_in)], writes=[emb])
            sp.dma([(skt[:], sk_in.broadcast_to([64, NH]))], writes=[skb])
            act.op(lambda e: e.activation(out=skt[:], in_=skt[:], func=AF.Exp), reads=[skb], writes=[skb])
            dve.op(lambda e: e.memset(skr[:], 0.0), writes=[skrb])
            for qd in range(4):
                for jj in range(4):
                    h = 8 * (qd // 2) + 2 * jj + (qd % 2)
                    dve.op(lambda e, h=h, qd=qd, jj=jj: e.tensor_scalar(
                        out=skr[:, qd * 4 + jj, :], in0=skr[:, qd * 4 + jj, :], scalar1=skt[:, h:h + 1],
                        scalar2=None, op0=ALU.add), reads=[skb], writes=[skrb])
            jobs = []
            for qd in range(4):
                for t in range(NBL):
                    p = t % 2
                    items = [(t, 0)]
                    if t > 0:
                        items.append((t - 1, 1 + 2 * p))
                    items.append((NBL + t, 2 + 2 * p))
                    for i_, (s_, mi) in enumerate(items):
                        jobs.append(dict(qd=qd, t=t, s=s_, mi=mi, g=qd * NBL + t, first=(i_ == 0),
                                         last=(i_ == len(items) - 1), qfirst=(t == 0 and i_ == 0)))

            def front(jb, n):
                qd, t, s_, mi = jb["qd"], jb["t"], jb["s"], jb["mi"]
                kv, hh = qd // 2, qd % 2
                pr = slice(hh * 64, hh * 64 + 64)
                if jb["qfirst"] and hh == 0:
                    sp.dma([(qt[:], QTA[4 * kv:4 * kv + 4].rearrange("c p t -> p c t"))], writes=[qtb])
                k = n % 3
                fns = [lambda e: e.matmul(psS[k][:].rearrange("p (a b) -> p a b", b=128),
                                          lhsT=kt[pr, kv, s_ * 128:(s_ + 1) * 128],
                                          rhs=qt[pr, :, t * 128:(t + 1) * 128], start=True, stop=False)]
                fns.append(lambda e: e.matmul(psS[k][:, :], lhsT=idt[:], rhs=m4[:, mi, :], start=False, stop=True))
                pe.group(fns, reads=[ktb, qtb, idb, m4b], writes=[psSb[k]])
                act.op(lambda e: e.activation(out=pt[k][:], in_=psS[k][:], func=AF.Exp, scale=0.125),
                       reads=[psSb[k]], writes=[ptb[k]])

            def back(jb, m):
                kv = jb["qd"] // 2
                k = m % 3
                o_ = jb["g"] % 2
                s_ = jb["s"]
                pe.group([lambda e: e.matmul(psO[o_][0:65, :], lhsT=vv[:, s_, kv * 65:(kv + 1) * 65], rhs=pt[k][:],
                                             start=jb["first"], stop=jb["last"])],
                         reads=[ptb[k], vvb], writes=[psOb[o_]])

            def tail_a(jb):
                o_ = jb["g"] % 2
                dve.op(lambda e: e.tensor_copy(out=osb[o_][:], in_=psO[o_][0:65, :]), reads=[psOb[o_]], writes=[osbb[o_]])

            def tail_b(jb):
                qd, t = jb["qd"], jb["t"]
                kv, hh = qd // 2, qd % 2
                qb_ = qd % 2
                o_ = jb["g"] % 2
                pe.group([lambda e: e.matmul(psL[0:64, :], lhsT=em[:], rhs=osb[o_][:], start=True, stop=True)],
                         reads=[emb, osbb[o_]], writes=[psLb])
                dve.op(lambda e: e.tensor_tensor(out=rinv[o_][:], in0=psL[0:64, :],
                                                 in1=skr[:, 4 * qd:4 * qd + 4, :].rearrange("p a b -> p (a b)"), op=ALU.add),
                       reads=[psLb, skrb], writes=[rinvb[o_]])
                act.op(lambda e: e.activation(out=rinv[o_][:], in_=rinv[o_][:], func=AF.Ln), reads=[rinvb[o_]], writes=[rinvb[o_]])
                act.op(lambda e: e.activation(out=rinv[o_][:], in_=rinv[o_][:], func=AF.Exp, scale=-1.0),
                       reads=[rinvb[o_]], writes=[rinvb[o_]])
                dve.op(lambda e: e.tensor_tensor(out=oaT[qb_][:, :, t * 128:(t + 1) * 128],
                                                 in0=osb[o_][0:64, :].rearrange("p (a b) -> p a b", b=128),
                                                 in1=rinv[o_][:].rearrange("p (a b) -> p a b", b=128), op=ALU.mult),
                       reads=[osbb[o_], rinvb[o_]], writes=[oaTb[qb_]])
                if t == NBL - 1:
                    pool.dma([(OAT[4 * kv + hq, hh * 64:hh * 64 + 64, :], oaT[qb_][:, hq, :]) for hq in range(4)],
                             reads=[oaTb[qb_]])

            run_pipeline(jobs, 2, front, back, tail_a, tail_b)
            P.close()

        if upto >= 6:
            P = Phase()
            wa, wab = P.sb("wa", [128, 8, D], BF16)
            wbt, wbb = P.sb("wb", [128, 8, D], BF16)
            oa, oab = P.sb("oa", [128, 8, 512], BF16, 2)
            obx, obxb = P.sb("obx", [128, 8, 512], BF16, 2)
            ga, gab = P.sb("ga", [128, 16, 512], BF16, 1)
            ga, gab = [ga, ga], [gab, gab]
            gbx, gbxb = P.sb("gbx", [128, 16, 512], BF16, 1)
            gbx, gbxb = [gbx, gbx], [gbxb, gbxb]
            m1, m1b = P.sb("m1", [128, 512], F32, 2)
            m2, m2b = P.sb("m2", [128, 512], F32, 2)
            mt, mtb = P.sb("mt", [128, 16, 512], BF16, 2)
            ps, psb = P.ps("ps", [128, 512], F32, 8)
            load_w_cast(wa, wab, wa_in, 2048, nchunk=8)
            load_w_cast(wbt, wbb, wb_in, 2048, nchunk=8)
            pi = 0
            for g in range(NGL):
                i = g % 2
                tsl = slice(g * 512, (g + 1) * 512)
                sp.dma([(oa[i][:], OAT[:, :, tsl].rearrange("c p t -> p c t"))], writes=[oab[i]])
                sp.dma([(obx[i][:], OBT[:, :, tsl].rearrange("c p t -> p c t"))], writes=[obxb[i]])
                sp.dma([(ga[i][:], GAT[:, :, tsl].rearrange("c p t -> p c t"))], writes=[gab[i]])
                sp.dma([(gbx[i][:], GBT[:, :, tsl].rearrange("c p t -> p c t"))], writes=[gbxb[i]])
                for dt in range(16):
                    k1 = pi % 8
                    k2 = (pi + 1) % 8
                    pi += 2
                    mi = dt % 2
                    pe.group([(lambda e, c=c, k1=k1: e.matmul(ps[k1][:], lhsT=wa[:, c, dt * 128:(dt + 1) * 128],
                                                              rhs=oa[i][:, c, :], start=(c == 0), stop=(c == 7)))
                              for c in range(8)], reads=[wab, oab[i]], writes=[psb[k1]])
                    pe.group([(lambda e, c=c, k2=k2: e.matmul(ps[k2][:], lhsT=wbt[:, c, dt * 128:(dt + 1) * 128],
                                                              rhs=obx[i][:, c, :], start=(c == 0), stop=(c == 7)))
                              for c in range(8)], reads=[wbb, obxb[i]], writes=[psb[k2]])
                    dve.op(lambda e, k1=k1, mi=mi, dt=dt: e.tensor_tensor(out=m1[mi][:], in0=ps[k1][:], in1=ga[i][:, dt, :],
                                                                          op=ALU.mult),
                           reads=[psb[k1], gab[i]], writes=[m1b[mi]])
                    dve.op(lambda e, k2=k2, mi=mi, dt=dt: e.tensor_tensor(out=m2[mi][:], in0=ps[k2][:], in1=gbx[i][:, dt, :],
                                                                          op=ALU.mult),
                           reads=[psb[k2], gbxb[i]], writes=[m2b[mi]])
                    pool.op(lambda e, mi=mi, dt=dt: e.tensor_tensor(out=mt[i][:, dt, :], in0=m1[mi][:], in1=m2[mi][:],
                                                                    op=ALU.add),
                            reads=[m1b[mi], m2b[mi]], writes=[mtb[i]])
                pool.dma([(MT[:, :, tsl].rearrange("c p t -> p c t"), mt[i][:])], reads=[mtb[i]])
            P.close()

        if upto >= 7:
            P = Phase()
            wo, wob = P.sb("wo", [128, 16, D], BF16)
            mt, mtb = P.sb("mt", [128, 16, 512], BF16, 1)
            mt, mtb = [mt, mt], [mtb, mtb]
            xt, xb = P.sb("xt", [128, D], F32, 2)
            x1, x1b = P.sb("x1", [128, D], F32, 2)
            gt, gb = P.sb("gt", [128, D], F32)
            ht, hb = P.sb("ht", [128, D], BF16, 2)
            junk, jb = P.sb("junk", [128, D], BF16)
            ss, ssb = P.sb("ss", [128, 1], F32, 2)
            epsc, epb = P.sb("epsc", [128, 1], F32)
            idt, idb = P.sb("idt", [128, 128], BF16)
            hTg, hTb = P.sb("hTg", [128, 16, 512], BF16, 2)
            ps, psb = P.ps("ps", [128, 512], F32, 4)
            pT, pTb = P.ps("pT", [128, 8, 128], BF16, 4)
            load_w_cast(wo, wob, wo_in, 2048)
            sp.dma([(gt[:], mn_in.broadcast_to([128, D]))], writes=[gb])
            sp.dma([(idt[:], id_in)], writes=[idb])
            dve.op(lambda e: e.memset(epsc[:], EPS), writes=[epb])
            cnt = [0]
            pi = 0
            for g in range(NGL):
                gi_ = g % 2
                sp.dma([(mt[gi_][:], MT[:, :, g * 512:(g + 1) * 512].rearrange("c p t -> p c t"))], writes=[mtb[gi_]])
                for bl in range(4):
                    s = g * 4 + bl
                    i = s % 2
                    sp.dma([(xt[i][:], x_in[s * 128:(s + 1) * 128, :])], writes=[xb[i]])
                    for cg in range(4):
                        k = pi % 4
                        pi += 1
                        pe.group([(lambda e, c=c, k=k: e.matmul(ps[k][:], lhsT=mt[gi_][:, c, bl * 128:(bl + 1) * 128],
                                                                rhs=wo[:, c, cg * 512:(cg + 1) * 512],
                                                                start=(c == 0), stop=(c == 15))) for c in range(16)],
                                 reads=[mtb[gi_], wob], writes=[psb[k]])
                        dve.op(lambda e, k=k, cg=cg: e.tensor_tensor(out=x1[i][:, cg * 512:(cg + 1) * 512], in0=ps[k][:],
                                                                     in1=xt[i][:, cg * 512:(cg + 1) * 512], op=ALU.add),
                               reads=[psb[k], xb[i]], writes=[x1b[i]])
                    pool.dma([(X1[s * 128:(s + 1) * 128, :], x1[i][:])], reads=[x1b[i]])
                    rmsnorm_to_bf16(P, x1[i], x1b[i], gt, gb, ht[i], hb[i], ss[i], ssb[i], junk, jb, epsc, epb)
                    transpose16(ht[i], hb[i], idt, idb, pT, pTb, hTg[gi_], hTb[gi_], bl, cnt)
                pool.dma([(H2T[:, :, g * 512:(g + 1) * 512].rearrange("c p t -> p c t"), hTg[gi_][:])],
                         reads=[hTb[gi_]])
            P.close()

        if upto >= 8:
            P = Phase()
            h2, h2b = P.sb("h2", [128, 16, 512], BF16)
            uT, uTb = P.sb("uT", [128, 64, 512], BF16)
            wu, wub = P.sb("wu", [128, 16, 256], BF16, 2)
            wd, wdb = P.sb("wd", [128, 8, 512], BF16, 3)
            rr, rrb = P.sb("rr", [128, 512], F32, 2)
            x2, x2b = P.sb("x2", [128, D], F32, 4)
            gt, gb = P.sb("gt", [128, D], F32)
            junk, jb = P.sb("junk", [128, 512], BF16)
            ss, ssb = P.sb("ss", [128, 4], F32, 2)
            s1, s1b = P.sb("s1", [128, 1], F32, 2)
            epsc, epb = P.sb("epsc", [128, 1], F32)
            ps, psb = P.ps("ps", [128, 512], F32, 4)
            pa, pab = P.ps("pa", [128, 512], F32, 4)
            sp.dma([(gt[:], fn_in.broadcast_to([128, D]))], writes=[gb])
            dve.op(lambda e: e.memset(epsc[:], EPS), writes=[epb])
            pi = 0
            wi_ = 0
            di_ = 0
            for g in range(NGL):
                sp.dma([(h2[:], H2T[:, :, g * 512:(g + 1) * 512].rearrange("c p t -> p c t"))], writes=[h2b])
                for bl in range(4):
                    s = g * 4 + bl
                    sp.dma([(x2[bl][:], X1[s * 128:(s + 1) * 128, :])], writes=[x2b[bl]])
                for sl in range(32):
                    w_ = wi_ % 2
                    wi_ += 1
                    sp.dma([(wu[w_][:], WUB[sl // 2][:, :, (sl % 2) * 256:(sl % 2) * 256 + 256])], writes=[wub[w_]])
                    for f4 in range(2):
                        ft = sl * 2 + f4
                        k = pi % 4
                        pi += 1
                        r_ = pi % 2
                        pe.group([(lambda e, c=c, k=k: e.matmul(ps[k][:], lhsT=wu[w_][:, c, f4 * 128:(f4 + 1) * 128],
                                                                rhs=h2[:, c, :], start=(c == 0), stop=(c == 15)))
                                  for c in range(16)], reads=[wub[w_], h2b], writes=[psb[k]])
                        act.op(lambda e, k=k, r_=r_: e.activation(out=rr[r_][:], in_=ps[k][:], func=AF.Relu),
                               reads=[psb[k]], writes=[rrb[r_]])
                        pool.op(lambda e, r_=r_, ft=ft: e.tensor_tensor(out=uT[:, ft, :], in0=rr[r_][:], in1=rr[r_][:],
                                                                        op=ALU.mult),
                                reads=[rrb[r_]], writes=[uTb])
                for ch in range(4):
                    for h8 in range(8):
                        d_ = di_ % 3
                        di_ += 1
                        hf, c0 = h8 // 2, (h8 % 2) * 8
                        sp.dma([(wd[d_][:, :, 0:256], WDB[2 * ch, hf][:, c0:c0 + 8, :]),
                                (wd[d_][:, :, 256:512], WDB[2 * ch + 1, hf][:, c0:c0 + 8, :])], writes=[wdb[d_]])
                        for bl in range(4):
                            fns = [(lambda e, c=c, bl=bl, h8=h8, d_=d_: e.matmul(
                                pa[bl][:, :], lhsT=uT[:, h8 * 8 + c, bl * 128:(bl + 1) * 128],
                                rhs=wd[d_][:, c, :], start=(h8 == 0 and c == 0), stop=(h8 == 7 and c == 7)))
                                for c in range(8)]
                            pe.group(fns, reads=[uTb, wdb[d_]], writes=[pab[bl]])
                    for bl in range(4):
                        dve.op(lambda e, bl=bl, ch=ch: e.tensor_tensor(out=x2[bl][:, ch * 512:(ch + 1) * 512],
                                                                       in0=pa[bl][:, :],
                                                                       in1=x2[bl][:, ch * 512:(ch + 1) * 512], op=ALU.add),
                               reads=[pab[bl], x2b[bl]], writes=[x2b[bl]])
                for bl in range(4):
                    s = g * 4 + bl
                    i = bl % 2
                    for q in range(4):
                        dve.op(lambda e, bl=bl, q=q, i=i: e.scalar_tensor_tensor(
                            out=junk[:], in0=x2[bl][:, q * 512:(q + 1) * 512], scalar=1.0,
                            in1=x2[bl][:, q * 512:(q + 1) * 512], op0=ALU.mult, op1=ALU.mult,
                            accum_out=ss[i][:, q:q + 1]), reads=[x2b[bl]], writes=[jb, ssb[i]])
                    dve.op(lambda e, i=i: e.tensor_reduce(out=s1[i][:], in_=ss[i][:], axis=mybir.AxisListType.X,
                                                          op=ALU.add), reads=[ssb[i]], writes=[s1b[i]])
                    act.op(lambda e, i=i: e.activation(out=s1[i][:], in_=s1[i][:], func=AF.Sqrt, scale=1.0 / D,
                                                       bias=epsc[:]), reads=[s1b[i], epb], writes=[s1b[i]])
                    dve.op(lambda e, i=i: e.reciprocal(out=s1[i][:], in_=s1[i][:]), reads=[s1b[i]], writes=[s1b[i]])
                    dve.op(lambda e, bl=bl, i=i: e.scalar_tensor_tensor(out=x2[bl][:], in0=x2[bl][:], scalar=s1[i][:],
                                                                        in1=gt[:], op0=ALU.mult, op1=ALU.mult),
                           reads=[x2b[bl], s1b[i], gb], writes=[x2b[bl]])
                    pool.dma([(y_out[s * 128:(s + 1) * 128, :], x2[bl][:])], reads=[x2b[bl]])
            P.close()
        if upto < 8:
            pass
    return nc


def _block_lists(nb_seq, typ):
    own, oth = [], []
    for j in range(nb_seq // 4):
        a = [4 * j, 4 * j + 3]
        b = [4 * j + 1, 4 * j + 2]
        own += a if typ == 0 else b
        oth += b if typ == 0 else a
    return own, oth


def _consts(typ):
    k = np.arange(128)[:, None]
    q = np.arange(128)[None, :]
    ident = np.eye(128, dtype=np.float32).astype(ml_dtypes.bfloat16)
    cmat = np.zeros((128, 4, 128), np.float32)
    cmat[:, 3, :] = np.eye(128)
    cmat[:, 0, :] = (k <= q)
    cmat[:, 1, :] = 1.0
    cmat[:, 2, :] = (k < 64)
    tri = np.where(k <= q, 0.0, NEG).astype(np.float32)
    band = np.where(k > q, 0.0, NEG).astype(np.float32)
    allm = np.full((128, 128), NEG, np.float32)
    fl = [(p ^ typ) for p in (0, 1)]
    swam = np.zeros((128, 2, 2, 128), np.float32)
    for p in (0, 1):
        swam[:, p, 0, :] = allm if fl[p] else band
        swam[:, p, 1, :] = band if fl[p] else allm
    flags = np.zeros((128, 6), np.float32)
    flags[:, 0], flags[:, 1] = fl[0], fl[1]
    flags[:, 2], flags[:, 3] = 1 - fl[0], 1 - fl[1]
    flags[:, 4], flags[:, 5] = (1 - fl[0]) * NEG, (1 - fl[1]) * NEG
    pidx = np.arange(128)
    invf = (10000.0 ** (-(np.arange(0, 64, 2, dtype=np.float64)) / 64.0))[pidx % 32]
    sgn = np.where((pidx % 64) < 32, -1.0, 1.0)
    ropec = np.stack([invf / (2 * np.pi), sgn * 2 * np.pi], axis=1).astype(np.float32)
    foxm = np.zeros((128, 2, 128), np.float32)
    for p in (0, 1):
        foxm[:, p, :] = 0.0 if fl[p] else NEG
    m5 = [tri, swam[:, 0, 0, :], swam[:, 0, 1, :], swam[:, 1, 0, :], swam[:, 1, 1, :]]
    swam4 = np.stack([np.tile(m_, (1, 4)) for m_ in m5], axis=1).astype(ml_dtypes.bfloat16)
    emat = np.zeros((65, 64), np.float32)
    emat[64, :] = 1.0
    return dict(ident=ident, cmat=cmat, foxmask=foxm.astype(ml_dtypes.bfloat16), emat=emat, swamask4=swam4, trimask=tri.astype(ml_dtypes.bfloat16),
                swamask=swam.astype(ml_dtypes.bfloat16), flags=flags, ropec=ropec)


def _perm_w_in(w_in):
    aq, ak, av = 0, 1024, 1152
    bq, bk, bv, bf = 1280, 2304, 3328, 4352
    ga, gb = 4368, 6416
    sw = np.concatenate([np.arange(32, 64), np.arange(0, 32)])
    cols_all = list(range(bk, bk + 1024))
    for kv in range(2):
        base = ak + kv * 64 + np.arange(64)
        bsw = ak + kv * 64 + sw
        cols_all += list(base) + list(base) + list(bsw) + list(bsw)
    cols_all += list(range(bv, bv + 1024)) + list(range(av, av + 128)) + list(range(bf, bf + 16))
    cols_loc = list(range(bq, bq + 1024))
    for j in range(8):
        t = aq + j * 128 + np.arange(128)
        ts = np.concatenate([aq + j * 128 + sw, aq + j * 128 + 64 + sw])
        cols_loc += list(t) + list(ts)
    cols_loc += list(range(ga, ga + 2048)) + list(range(gb, gb + 2048))
    return np.ascontiguousarray(w_in[:, cols_all]), np.ascontiguousarray(w_in[:, cols_loc])


def make_inputs(x, positions, attn_norm, w_in, fox_f_bias, swa_sinks, w_branch_swa, w_branch_fox, w_out,
                mlp_norm, w_up, w_down, final_norm, n_cores=8):
    B, S, _ = x.shape
    nb_seq = S // 128
    w_all, w_loc = _perm_w_in(np.asarray(w_in[0]))
    shared = dict(attn_norm=np.asarray(attn_norm[0:1]), w_all=w_all, w_loc=w_loc, fbias=np.asarray(fox_f_bias[0:1]),
                  sinks=np.asarray(swa_sinks[0:1]), wa=np.asarray(w_branch_swa[0]), wb=np.asarray(w_branch_fox[0]),
                  wout=np.asarray(w_out[0]), mlp_norm=np.asarray(mlp_norm[0:1]), w_up=np.asarray(w_up[0]),
                  w_down=np.asarray(w_down[0]), final_norm=np.asarray(final_norm).reshape(1, D))
    in_maps, owns = [], []
    for c in range(n_cores):
        b, typ = c // 2, c % 2
        own, oth = _block_lists(nb_seq, typ)
        order = own + oth
        xb = np.asarray(x[b]).reshape(nb_seq, 128, D)[order].reshape(S, D)
        pb = np.asarray(positions[b]).reshape(nb_seq, 128)[order].reshape(1, S).astype(np.int32)
        m = dict(shared)
        m.update(_consts(typ))
        m["x"] = np.ascontiguousarray(xb)
        m["pos"] = np.ascontiguousarray(pb)
        in_maps.append(m)
        owns.append((b, own))
    return in_maps, owns


_NC_CACHE = {}


def kernel(x, positions, attn_norm, w_in, fox_f_bias, swa_sinks, w_branch_swa, w_branch_fox, w_out,
           mlp_norm, w_up, w_down, final_norm):
    x = np.asarray(x)
    B, S, _ = x.shape
    NBL = S // 128 // 2
    in_maps, owns = make_inputs(x, positions, attn_norm, w_in, fox_f_bias, swa_sinks, w_branch_swa, w_branch_fox,
                                w_out, mlp_norm, w_up, w_down, final_norm)
    if NBL not in _NC_CACHE:
        _NC_CACHE[NBL] = build(NBL)
    res = run_bass_kernel_spmd(_NC_CACHE[NBL], in_maps, core_ids=list(range(8)))
    out = np.zeros((B, S, D), np.float32)
    o4 = out.reshape(B, S // 128, 128, D)
    for c, (b, own) in enumerate(owns):
        y = np.asarray(res.results[c]["y"]).reshape(NBL, 128, D)
        o4[b, own] = y
    return out
```

```python
import math
from contextlib import ExitStack
import numpy as np
import ml_dtypes
import concourse.bass as bass
import concourse.mybir as mybir
from concourse.bass_utils import run_bass_kernel_spmd

F32 = mybir.dt.float32
BF16 = mybir.dt.bfloat16
I32 = mybir.dt.int32
AF = mybir.ActivationFunctionType
ALU = mybir.AluOpType

D = 2048
DFF = 8192
NH = 16
HD = 64
NEG = -30000.0
MAGIC = 12582912.0
EPS = 1e-6


class _Rec:
    def __getattr__(self, name):
        def f(*a, **kw):
            return (name, a, kw)
        return f


REC = _Rec()


def _play(e, call):
    return getattr(e, call[0])(*call[1], **call[2])


class Buf:
    __slots__ = ("name", "w", "r", "dsem", "dcnt")

    def __init__(self, name=""):
        self.name = name
        self.w = None
        self.r = {}
        self.dsem = None
        self.dcnt = 0


class Ctx:
    def __init__(self, nc, stack):
        self.nc = nc
        self.stack = stack
        self.sems = {}
        self.engs = {}
        self.free_dsems = []
        self.dvals = {}

    def sem(self, key):
        if key not in self.sems:
            self.sems[key] = self.stack.enter_context(self.nc.semaphore(key))
        return self.sems[key]

    def new_dsem(self):
        if self.free_dsems:
            return self.free_dsems.pop()
        k = "d%d" % len(self.sems)
        self.sem(k)
        self.dvals[k] = 0
        return k

    def engine(self, name, eng):
        e = Eng(self, name, eng)
        self.engs[name] = e
        return e

    def end_phase(self, bufs):
        sp = self.engs["sp"]
        toks = [(k, v) for k, v in self.dvals.items() if v]
        for e in self.engs.values():
            if e.n:
                toks.append((e.key, e.n))
        for e in self.engs.values():
            e._wait(toks)
        for b in bufs:
            if b.dsem is not None:
                self.free_dsems.append(b.dsem)
                b.dsem = None

    def emit(self):
        nc = self.nc
        E = self.engs
        with nc.Block() as block:
            @block.sync
            def _(e):
                E["sp"].replay(e)

            @block.tensor
            def _(e):
                E["pe"].replay(e)

            @block.scalar
            def _(e):
                E["act"].replay(e)

            @block.vector
            def _(e):
                E["dve"].replay(e)

            @block.gpsimd
            def _(e):
                E["pool"].replay(e)


class Eng:
    def __init__(self, ctx, name, eng):
        self.ctx = ctx
        self.name = name
        self.key = "s_" + name
        ctx.sem(self.key)
        self.n = 0
        self.seen = {}
        self.prog = []
        self.inorder = (name == "pe")
        self.nops = 0

    def _wait(self, toks):
        need = {}
        for t in toks:
            if t is None:
                continue
            k, v = t
            if v > need.get(k, 0):
                need[k] = v
        for k, v in need.items():
            if self.seen.get(k, 0) >= v:
                continue
            if k == self.key and (self.inorder or v <= self.n - 1):
                continue
            sh = self.ctx.sems[k]
            self.prog.append(lambda e, sh=sh, v=v: e.wait_ge(sh, v))
            self.seen[k] = v

    @staticmethod
    def _deps(reads, writes):
        toks = []
        for b in reads:
            toks.append(b.w)
        for b in writes:
            toks.append(b.w)
            toks.extend(b.r.items())
        return toks

    @staticmethod
    def _mark(tok, reads, writes):
        k, v = tok
        for b in reads:
            if b.r.get(k, 0) < v:
                b.r[k] = v
        for b in writes:
            b.w = tok
            b.r = {}

    def op(self, fn, reads=(), writes=()):
        self._wait(self._deps(reads, writes))
        self.n += 1
        self.nops += 1
        sh = self.ctx.sems[self.key]
        call = fn(REC)
        self.prog.append(lambda e, call=call, sh=sh: _play(e, call).then_inc(sh, 1))
        tok = (self.key, self.n)
        self._mark(tok, reads, writes)
        return tok

    def group(self, fns, reads=(), writes=()):
        self._wait(self._deps(reads, writes))
        calls = [fn(REC) for fn in fns]
        for call in calls[:-1]:
            self.prog.append(lambda e, call=call: _play(e, call))
        self.nops += len(calls)
        self.n += 1
        sh = self.ctx.sems[self.key]
        self.prog.append(lambda e, call=calls[-1], sh=sh: _play(e, call).then_inc(sh, 1))
        tok = (self.key, self.n)
        self._mark(tok, reads, writes)
        return tok

    def dma(self, pairs, reads=(), writes=(), dbuf=None):
        if dbuf is None:
            dbuf = writes[0] if writes else reads[0]
        if dbuf.dsem is None:
            dbuf.dsem = self.ctx.new_dsem()
            dbuf.dcnt = self.ctx.dvals[dbuf.dsem]
        toks = self._deps(reads, writes)
        if dbuf.dcnt:
            toks.append((dbuf.dsem, dbuf.dcnt))
        self._wait(toks)
        sh = self.ctx.sems[dbuf.dsem]
        for (o, i) in pairs:
            self.prog.append(lambda e, o=o, i=i, sh=sh: e.dma_start(out=o, in_=i).then_inc(sh, 16))
        dbuf.dcnt += 16 * len(pairs)
        self.ctx.dvals[dbuf.dsem] = dbuf.dcnt
        tok = (dbuf.dsem, dbuf.dcnt)
        self._mark(tok, reads, writes)
        return tok

    def replay(self, e):
        for f in self.prog:
            f(e)
        self.prog = []


def build(NBL, debug=False, upto=99):
    NBA = 2 * NBL
    TL = NBL * 128
    TA = NBA * 128
    NGL = NBL // 4
    NGA = NBA // 4
    nc = bass.Bass("TRN2", target_bir_lowering=False)

    def din(name, shape, dt):
        return nc.dram_tensor(name, shape, dt, kind="ExternalInput").ap()

    def scr(name, shape, dt):
        return nc.dram_tensor(name, shape, dt, kind="ExternalOutput" if debug else "Internal").ap()

    x_in = din("x", [TA, D], F32)
    pos_in = din("pos", [1, TA], I32)
    an_in = din("attn_norm", [1, D], F32)
    wall_in = din("w_all", [D, 2704], F32)
    wloc_in = din("w_loc", [D, 7168], F32)
    fb_in = din("fbias", [1, NH], F32)
    sk_in = din("sinks", [1, NH], F32)
    wa_in = din("wa", [1024, D], F32)
    wb_in = din("wb", [1024, D], F32)
    wo_in = din("wout", [D, D], F32)
    mn_in = din("mlp_norm", [1, D], F32)
    wu_in = din("w_up", [D, DFF], F32)
    wd_in = din("w_down", [DFF, D], F32)
    fn_in = din("final_norm", [1, D], F32)
    id_in = din("ident", [128, 128], BF16)
    cm_in = din("cmat", [128, 4, 128], F32)
    tm_in = din("trimask", [128, 128], BF16)
    sm_in = din("swamask", [128, 2, 2, 128], BF16)
    fl_in = din("flags", [128, 6], F32)
    rc_in = din("ropec", [128, 2], F32)
    fm_in = din("foxmask", [128, 2, 128], BF16)
    m4_in = din("swamask4", [128, 5, 512], BF16)
    em_in = din("emat", [65, 64], F32)
    y_out = nc.dram_tensor("y", [TL, D], F32, kind="ExternalOutput").ap()

    HT = scr("HT", [16, 128, TA], BF16)
    KTF = scr("KTF", [8, 2, 65, TA], BF16)
    KTA = scr("KTA", [2, 128, TA], BF16)
    VF = scr("VF", [NBA, 128, NH * 65], BF16)
    VA = scr("VA", [NBA, 128, 2 * 65], BF16)
    QTF = scr("QTF", [8, 2, 65, TL], BF16)
    WUB = scr("WUB", [16, 128, 16, 512], BF16)
    WDB = scr("WDB", [8, 4, 128, 16, 256], BF16)
    QTA = scr("QTA", [8, 128, TL], BF16)
    GAT = scr("GAT", [16, 128, TL], BF16)
    GBT = scr("GBT", [16, 128, TL], BF16)
    NCS = scr("NCS", [128, NBA * NH], F32)
    NCT = scr("NCT", [128, NBL * NH], F32)
    OBT = scr("OBT", [8, 128, TL], BF16)
    OAT = scr("OAT", [8, 128, TL], BF16)
    MT = scr("MT", [16, 128, TL], BF16)
    X1 = scr("X1", [TL, D], F32)
    H2T = scr("H2T", [16, 128, TL], BF16)

    with ExitStack() as gst:
        ctx = Ctx(nc, gst)
        sp = ctx.engine("sp", nc.sync)
        pe = ctx.engine("pe", nc.tensor)
        act = ctx.engine("act", nc.scalar)
        dve = ctx.engine("dve", nc.vector)
        pool = ctx.engine("pool", nc.gpsimd)

        class Phase:
            _n = [0]

            def __init__(self):
                self.st = ExitStack()
                self.bufs = []
                Phase._n[0] += 1
                self.pfx = "p%d_" % Phase._n[0]

            def sb(self, name, shape, dt, nb=1):
                ts = [self.st.enter_context(nc.sbuf_tensor(self.pfx + "%s_%d" % (name, i), shape, dt)) for i in range(nb)]
                bs = [Buf("%s%d" % (name, i)) for i in range(nb)]
                self.bufs.extend(bs)
                return (ts, bs) if nb > 1 else (ts[0], bs[0])

            def ps(self, name, shape, dt, nb=1):
                ts = [self.st.enter_context(nc.psum_tensor(self.pfx + "%s_%d" % (name, i), shape, dt)) for i in range(nb)]
                bs = [Buf("%s%d" % (name, i)) for i in range(nb)]
                self.bufs.extend(bs)
                return (ts, bs) if nb > 1 else (ts[0], bs[0])

            def close(self):
                ctx.end_phase(self.bufs)
                ctx.emit()
                self.st.close()

        def rmsnorm_to_bf16(P, xt, xb, gt, gb, ht, hb, ss, ssb, junk, jb, epsc, epb):
            dve.op(lambda e: e.scalar_tensor_tensor(out=junk[:], in0=xt[:], scalar=1.0, in1=xt[:], op0=ALU.mult,
                                                    op1=ALU.mult, accum_out=ss[:]),
                   reads=[xb], writes=[jb, ssb])
            act.op(lambda e: e.activation(out=ss[:], in_=ss[:], func=AF.Sqrt, scale=1.0 / D, bias=epsc[:]),
                   reads=[ssb, epb], writes=[ssb])
            dve.op(lambda e: e.reciprocal(out=ss[:], in_=ss[:]), reads=[ssb], writes=[ssb])
            dve.op(lambda e: e.scalar_tensor_tensor(out=ht[:], in0=xt[:], scalar=ss[:], in1=gt[:], op0=ALU.mult,
                                                    op1=ALU.mult),
                   reads=[xb, ssb, gb], writes=[hb])

        def transpose16(ht, hb, idt, idb, pT, pTb, dst, dstb, blk, cnt):
            for hf in range(2):
                k = cnt[0] % len(pT)
                cnt[0] += 1
                pe.group([(lambda e, c=c, k=k: e.transpose(out=pT[k][:, c % 8, :], in_=ht[:, c * 128:(c + 1) * 128],
                                                            identity=idt[:])) for c in range(hf * 8, hf * 8 + 8)],
                         reads=[hb, idb], writes=[pTb[k]])
                act.op(lambda e, k=k, hf=hf: e.activation(out=dst[:, hf * 8:hf * 8 + 8, blk * 128:(blk + 1) * 128],
                                                          in_=pT[k][:], func=AF.Copy),
                       reads=[pTb[k]], writes=[dstb])

        if upto >= 1:
            P = Phase()
            xt, xb = P.sb("xt", [128, D], F32, 2)
            gt, gb = P.sb("gt", [128, D], F32)
            ht, hb = P.sb("ht", [128, D], BF16, 2)
            junk, jb = P.sb("junk", [128, D], BF16)
            ss, ssb = P.sb("ss", [128, 1], F32, 2)
            epsc, epb = P.sb("epsc", [128, 1], F32)
            idt, idb = P.sb("idt", [128, 128], BF16)
            hTg, hTb = P.sb("hTg", [128, 16, 512], BF16, 2)
            pT, pTb = P.ps("pT", [128, 8, 128], BF16, 4)
            sp.dma([(gt[:], an_in.broadcast_to([128, D]))], writes=[gb])
            sp.dma([(idt[:], id_in)], writes=[idb])
            dve.op(lambda e: e.memset(epsc[:], EPS), writes=[epb])
            cnt = [0]
            for s in range(NBA):
                i = s % 2
                g = s // 4
                sp.dma([(xt[i][:], x_in[s * 128:(s + 1) * 128, :])], writes=[xb[i]])
                rmsnorm_to_bf16(P, xt[i], xb[i], gt, gb, ht[i], hb[i], ss[i], ssb[i], junk, jb, epsc, epb)
                transpose16(ht[i], hb[i], idt, idb, pT, pTb, hTg[g % 2], hTb[g % 2], s % 4, cnt)
                if s % 4 == 3:
                    pool.dma([(HT[:, :, g * 512:(g + 1) * 512].rearrange("c p t -> p c t"), hTg[g % 2][:])],
                             reads=[hTb[g % 2]])
            P.close()

        def rope_tables(P, T, g):
            (posi, posib, u, ub, t1, t1b, kk, kkb, dd, ddb, cosT, cosb, sinT, sinb, rc, rcb) = T
            sp.dma([(posi[:], pos_in[:, g * 512:(g + 1) * 512].broadcast_to([128, 512]))], writes=[posib])
            dve.op(lambda e: e.tensor_copy(out=u[:], in_=posi[:]), reads=[posib], writes=[ub])
            dve.op(lambda e: e.tensor_scalar(out=u[:], in0=u[:], scalar1=rc[:, 0:1], scalar2=None, op0=ALU.mult),
                   reads=[ub, rcb], writes=[ub])
            dve.op(lambda e: e.tensor_scalar(out=t1[:], in0=u[:], scalar1=MAGIC, scalar2=None, op0=ALU.add),
                   reads=[ub], writes=[t1b])
            dve.op(lambda e: e.tensor_scalar(out=kk[:], in0=t1[:], scalar1=MAGIC, scalar2=None, op0=ALU.subtract),
                   reads=[t1b], writes=[kkb])
            dve.op(lambda e: e.tensor_tensor(out=dd[:], in0=u[:], in1=kk[:], op=ALU.subtract),
                   reads=[ub, kkb], writes=[ddb])
            act.op(lambda e: e.activation(out=sinT[:], in_=dd[:], func=AF.Sin, scale=rc[:, 1:2]),
                   reads=[ddb, rcb], writes=[sinb])
            dve.op(lambda e: e.tensor_scalar(out=t1[:], in0=u[:], scalar1=0.25, scalar2=MAGIC, op0=ALU.add,
                                             op1=ALU.add), reads=[ub], writes=[t1b])
            dve.op(lambda e: e.tensor_scalar(out=kk[:], in0=t1[:], scalar1=MAGIC, scalar2=None, op0=ALU.subtract),
                   reads=[t1b], writes=[kkb])
            dve.op(lambda e: e.scalar_tensor_tensor(out=dd[:], in0=u[:], scalar=0.25, in1=kk[:], op0=ALU.add,
                                                    op1=ALU.subtract), reads=[ub, kkb], writes=[ddb])
            act.op(lambda e: e.activation(out=cosT[:], in_=dd[:], func=AF.Sin, scale=2.0 * math.pi),
                   reads=[ddb], writes=[cosb])

        def rope_alloc(P):
            posi, posib = P.sb("posi", [128, 512], I32)
            u, ub = P.sb("ru", [128, 512], F32)
            t1, t1b = P.sb("rt1", [128, 512], F32)
            kk, kkb = t1, t1b
            dd, ddb = P.sb("rdd", [128, 512], F32)
            cosT, cosb = P.sb("cosT", [128, 512], F32)
            sinT, sinb = P.sb("sinT", [128, 512], F32)
            rc, rcb = P.sb("rc", [128, 2], F32)
            sp.dma([(rc[:], rc_in)], writes=[rcb])
            return (posi, posib, u, ub, t1, t1b, kk, kkb, dd, ddb, cosT, cosb, sinT, sinb, rc, rcb)

        def load_w_cast(dst, dstb, src_ap, ncol, c0=0, nchunk=16):
            pairs = []
            step = max(1, 4 // max(1, ncol // 512))
            for c in range(0, nchunk, step):
                pairs.append((dst[:, c0 + c:c0 + c + step, 0:ncol],
                              src_ap[c * 128:(c + step) * 128, :].rearrange("(c p) n -> p c n", p=128)))
            pool.dma(pairs, writes=[dstb])

        def ft_tile(ps, psb, w, wb_, wcol, hTg, hTgb):
            pe.group([(lambda e, c=c: e.matmul(ps[:], lhsT=w[:, c, wcol:wcol + 128], rhs=hTg[:, c, :],
                                               start=(c == 0), stop=(c == 15))) for c in range(16)],
                     reads=[wb_, hTgb], writes=[psb])

        def rope_apply(ps_t, psb_t, ps_s, psb_s, T, tmp1, tmp1b, tmp2, tmp2b, dst_ap, dstb):
            cosT, cosb, sinT, sinb = T[10], T[11], T[12], T[13]
            dve.op(lambda e: e.tensor_tensor(out=tmp1[:], in0=ps_t[:], in1=cosT[:], op=ALU.mult),
                   reads=[psb_t, cosb], writes=[tmp1b])
            dve.op(lambda e: e.tensor_tensor(out=tmp2[:], in0=ps_s[:], in1=sinT[:], op=ALU.mult),
                   reads=[psb_s, sinb], writes=[tmp2b])
            pool.op(lambda e: e.tensor_tensor(out=dst_ap, in0=tmp1[:], in1=tmp2[:], op=ALU.add),
                    reads=[tmp1b, tmp2b], writes=[dstb])

        if upto >= 2:
            P = Phase()
            w, wb_ = P.sb("wall", [128, 16, 2704], BF16)
            hTg, hTgb = P.sb("hTg", [128, 16, 512], BF16, 2)
            T = rope_alloc(P)
            tmp1, tmp1b = P.sb("tmp1", [128, 512], F32)
            tmp2, tmp2b = P.sb("tmp2", [128, 512], F32)
            kst, kstb = P.sb("kst", [128, 8, 512], BF16, 1)
            kst, kstb = [kst, kst], [kstb, kstb]
            kast, kastb = P.sb("kast", [128, 2, 512], BF16, 2)
            vst, vstb = P.sb("vst", [128, 4, NH * 65], BF16, 1)
            vst, vstb = [vst, vst], [vstb, vstb]
            vast, vastb = P.sb("vast", [128, 4, 2 * 65], BF16, 2)
            fbt, fbb = P.sb("fbt", [128, NH], F32)
            nl, nlb = P.sb("nl", [128, NBA * NH], F32)
            zz, zzb = P.sb("zz", [128, NH], F32)
            cm, cmb = P.sb("cm", [128, 4, 128], F32)
            fl, flb = P.sb("fl", [128, 6], F32)
            wi, wib = P.sb("wi", [128, NBA * NH], F32)
            tot, totb = P.sb("tot", [128, NBA * NH], F32)
            hal, halb = P.sb("hal", [128, NBL * NH], F32)
            run, runb = P.sb("run", [128, NH], F32)
            ca, cab = P.sb("ca", [128, NH], F32)
            cb2, cb2b = P.sb("cb2", [128, NH], F32)
            ncs, ncsb = P.sb("ncs", [128, NBA * NH], F32)
            nct, nctb = P.sb("nct", [128, NBL * NH], F32)
            ps, psb = P.ps("ps", [128, 512], F32, 8)
            for i in range(2):
                dve.op(lambda e, i=i: e.memset(vst[i][:], 1.0), writes=[vstb[i]])
                dve.op(lambda e, i=i: e.memset(vast[i][:], 1.0), writes=[vastb[i]])
            onesr, onesrb = P.sb("onesr", [16, TA], BF16)
            shT, shTb = P.sb("shT", [16, TL], BF16)
            vsh, vshb = P.sb("vsh", [128, NH], F32, 2)
            dve.op(lambda e: e.memset(onesr[:], 1.0), writes=[onesrb])
            sp.dma([(KTF.rearrange("j h d t -> (j h) d t")[:, 64, :], onesr[:])], reads=[onesrb])
            sp.dma([(fbt[:], fb_in.broadcast_to([128, NH]))], writes=[fbb])
            sp.dma([(cm[:], cm_in)], writes=[cmb])
            sp.dma([(fl[:], fl_in)], writes=[flb])
            for c0 in range(0, 2704, 512):
                n = min(512, 2704 - c0)
                pairs = []
                for c in range(0, 16, 4):
                    pairs.append((w[:, c:c + 4, c0:c0 + n],
                                  wall_in[c * 128:(c + 4) * 128, c0:c0 + n].rearrange("(c p) n -> p c n", p=128)))
                pool.dma(pairs, writes=[wb_])
            pi = [0]

            def nps():
                k = pi[0] % 8
                pi[0] += 1
                return k

            for g in range(NGA):
                hb_i = g % 2
                sp.dma([(hTg[hb_i][:], HT[:, :, g * 512:(g + 1) * 512].rearrange("c p t -> p c t"))],
                       writes=[hTgb[hb_i]])
                rope_tables(P, T, g)
                for j in range(8):
                    k = nps()
                    ft_tile(ps[k], psb[k], w, wb_, j * 128, hTg[hb_i], hTgb[hb_i])
                    act.op(lambda e, k=k, j=j: e.activation(out=kst[hb_i][:, j, :], in_=ps[k][:], func=AF.Copy),
                           reads=[psb[k]], writes=[kstb[hb_i]])
                pool.dma([(KTF[:, hh, 0:64, g * 512:(g + 1) * 512].rearrange("j d t -> d j t"),
                           kst[hb_i][hh * 64:(hh + 1) * 64, :, :]) for hh in range(2)], reads=[kstb[hb_i]])
                for kv in range(2):
                    k1 = nps()
                    ft_tile(ps[k1], psb[k1], w, wb_, 1024 + kv * 256, hTg[hb_i], hTgb[hb_i])
                    k2 = nps()
                    ft_tile(ps[k2], psb[k2], w, wb_, 1024 + kv * 256 + 128, hTg[hb_i], hTgb[hb_i])
                    rope_apply(ps[k1], psb[k1], ps[k2], psb[k2], T, tmp1, tmp1b, tmp2, tmp2b,
                               kast[hb_i][:, kv, :], kastb[hb_i])
                pool.dma([(KTA[:, :, g * 512:(g + 1) * 512].rearrange("c p t -> p c t"), kast[hb_i][:])],
                         reads=[kastb[hb_i]])
                for bl in range(4):
                    s = g * 4 + bl
                    for hf in range(2):
                        k = nps()
                        pe.group([(lambda e, c=c, k=k: e.matmul(ps[k][:], lhsT=hTg[hb_i][:, c, bl * 128:(bl + 1) * 128],
                                                                rhs=w[:, c, 1536 + hf * 512:1536 + (hf + 1) * 512],
                                                                start=(c == 0), stop=(c == 15))) for c in range(16)],
                                 reads=[wb_, hTgb[hb_i]], writes=[psb[k]])
                        dve.op(lambda e, k=k, hf=hf: e.tensor_copy(
                            out=vst[hb_i][:, bl, hf * 8 * 65:(hf + 1) * 8 * 65].rearrange("p (h d) -> p h d", d=65)[:, :, 0:64],
                            in_=ps[k][:].rearrange("p (h d) -> p h d", d=64)), reads=[psb[k]], writes=[vstb[hb_i]])
                    k = nps()
                    pe.group([(lambda e, c=c, k=k: e.matmul(ps[k][:, 0:144], lhsT=hTg[hb_i][:, c, bl * 128:(bl + 1) * 128],
                                                            rhs=w[:, c, 2560:2704],
                                                            start=(c == 0), stop=(c == 15))) for c in range(16)],
                             reads=[wb_, hTgb[hb_i]], writes=[psb[k]])
                    dve.op(lambda e, k=k: e.tensor_copy(
                        out=vast[hb_i][:, bl, :].rearrange("p (h d) -> p h d", d=65)[:, :, 0:64],
                        in_=ps[k][:, 0:128].rearrange("p (h d) -> p h d", d=64)), reads=[psb[k]], writes=[vastb[hb_i]])
                    dve.op(lambda e, k=k: e.tensor_tensor(out=zz[:], in0=ps[k][:, 128:144], in1=fbt[:], op=ALU.add),
                           reads=[psb[k], fbb], writes=[zzb])
                    act.op(lambda e: e.activation(out=zz[:], in_=zz[:], func=AF.Exp, scale=-1.0),
                           reads=[zzb], writes=[zzb])
                    act.op(lambda e, s=s: e.activation(out=nl[:, s * NH:(s + 1) * NH], in_=zz[:], func=AF.Ln, bias=1.0),
                           reads=[zzb], writes=[nlb])
                pool.dma([(VF[g * 4:(g + 1) * 4].rearrange("s p n -> p s n"), vst[hb_i][:])], reads=[vstb[hb_i]])
                pool.dma([(VA[g * 4:(g + 1) * 4].rearrange("s p n -> p s n"), vast[hb_i][:])], reads=[vastb[hb_i]])
            for m, (dst, dstb, nsl) in enumerate([(wi, wib, NBA), (tot, totb, NBA), (hal, halb, NBL)]):
                for c0 in range(0, nsl * NH, 512):
                    n = min(512, nsl * NH - c0)
                    k = nps()
                    pe.group([lambda e, k=k, m=m, c0=c0, n=n: e.matmul(ps[k][:, 0:n], lhsT=cm[:, m, :],
                                                                      rhs=nl[:, c0:c0 + n], start=True, stop=True)],
                             reads=[cmb, nlb], writes=[psb[k]])
                    dve.op(lambda e, k=k, dst=dst, c0=c0, n=n: e.tensor_copy(out=dst[:, c0:c0 + n], in_=ps[k][:, 0:n]),
                           reads=[psb[k]], writes=[dstb])
            dve.op(lambda e: e.memset(run[:], 0.0), writes=[runb])
            OW = NBL * NH
            for t in range(NBL):
                p = t % 2
                so = slice(t * NH, (t + 1) * NH)
                st_ = slice(OW + t * NH, OW + (t + 1) * NH)
                dve.op(lambda e, p=p, st_=st_: e.scalar_tensor_tensor(out=ca[:], in0=tot[:, st_], scalar=fl[:, p:p + 1],
                                                                      in1=run[:], op0=ALU.mult, op1=ALU.add),
                       reads=[totb, flb, runb], writes=[cab])
                dve.op(lambda e, p=p, so=so: e.scalar_tensor_tensor(out=cb2[:], in0=tot[:, so], scalar=fl[:, 2 + p:3 + p],
                                                                    in1=run[:], op0=ALU.mult, op1=ALU.add),
                       reads=[totb, flb, runb], writes=[cb2b])
                dve.op(lambda e, so=so: e.tensor_tensor(out=ncs[:, so], in0=wi[:, so], in1=ca[:], op=ALU.add),
                       reads=[wib, cab], writes=[ncsb])
                dve.op(lambda e, so=so: e.scalar_tensor_tensor(out=nct[:, so], in0=hal[:, so], scalar=-1.0, in1=ca[:],
                                                               op0=ALU.mult, op1=ALU.subtract),
                       reads=[halb, cab], writes=[nctb])
                dve.op(lambda e, st_=st_: e.tensor_tensor(out=ncs[:, st_], in0=wi[:, st_], in1=cb2[:], op=ALU.add),
                       reads=[wib, cb2b], writes=[ncsb])
                dve.op(lambda e, so=so: e.tensor_tensor(out=run[:], in0=run[:], in1=tot[:, so], op=ALU.add),
                       reads=[runb, totb], writes=[runb])
                dve.op(lambda e, st_=st_: e.tensor_tensor(out=run[:], in0=run[:], in1=tot[:, st_], op=ALU.add),
                       reads=[runb, totb], writes=[runb])
            for t in range(NBL):
                rf = min(4 * (t // 4) + 2, NBL - 1)
                so = slice(t * NH, (t + 1) * NH)
                v_ = t % 2
                k = nps()
                dve.op(lambda e, so=so, rf=rf, v_=v_: e.tensor_tensor(out=vsh[v_][:], in0=ncs[:, so],
                                                                      in1=nct[:, rf * NH:(rf + 1) * NH], op=ALU.add),
                       reads=[ncsb, nctb], writes=[vshb[v_]])
                pe.group([lambda e, k=k, v_=v_: e.transpose(out=ps[k][0:16, 0:128], in_=vsh[v_][:], identity=cm[:, 3, :])],
                         reads=[vshb[v_], cmb], writes=[psb[k]])
                dve.op(lambda e, k=k, t=t: e.tensor_scalar(out=shT[:, t * 128:(t + 1) * 128], in0=ps[k][0:16, 0:128],
                                                           scalar1=-8.0, scalar2=None, op0=ALU.mult),
                       reads=[psb[k]], writes=[shTb])
            sp.dma([(QTF.rearrange("j h d t -> (j h) d t")[:, 64, :], shT[:])], reads=[shTb])
            sp.dma([(NCS, ncs[:])], reads=[ncsb])
            sp.dma([(NCT, nct[:])], reads=[nctb])
            P.close()

        if upto >= 3:
            P = Phase()
            w, wb_ = P.sb("wloc", [128, 16, 1024], BF16, 2)
            hTg, hTgb = P.sb("hTg", [128, 16, 512], BF16, 2)
            T = rope_alloc(P)
            tmp1, tmp1b = P.sb("tmp1", [128, 512], F32)
            tmp2, tmp2b = P.sb("tmp2", [128, 512], F32)
            stg, stgb = P.sb("stg", [128, 8, 512], BF16, 2)
            ps, psb = P.ps("ps", [128, 512], F32, 8)
            pi = [0]
            gi = 0
            dsts = [(QTF, 0, "copy"), (QTA, 0, "rope"), (QTA, 4, "rope"), (GAT, 0, "sig"), (GAT, 8, "sig"),
                    (GBT, 0, "sig"), (GBT, 8, "sig")]
            load_w_cast(w[0], wb_[0], wloc_in[:, 0:1024], 1024)
            for sl in range(7):
                wi_ = sl % 2
                if sl + 1 < 7:
                    load_w_cast(w[1 - wi_], wb_[1 - wi_], wloc_in[:, (sl + 1) * 1024:(sl + 2) * 1024], 1024)
                dst, t0, kind = dsts[sl]
                nt = 4 if kind == "rope" else 8
                for g in range(NGL):
                    hb_i = gi % 2
                    gi += 1
                    sp.dma([(hTg[hb_i][:], HT[:, :, g * 512:(g + 1) * 512].rearrange("c p t -> p c t"))],
                           writes=[hTgb[hb_i]])
                    if kind == "rope":
                        rope_tables(P, T, g)
                    for j in range(nt):
                        if kind == "rope":
                            k1 = pi[0] % 8
                            k2 = (pi[0] + 1) % 8
                            pi[0] += 2
                            ft_tile(ps[k1], psb[k1], w[wi_], wb_[wi_], j * 256, hTg[hb_i], hTgb[hb_i])
                            ft_tile(ps[k2], psb[k2], w[wi_], wb_[wi_], j * 256 + 128, hTg[hb_i], hTgb[hb_i])
                            rope_apply(ps[k1], psb[k1], ps[k2], psb[k2], T, tmp1, tmp1b, tmp2, tmp2b,
                                       stg[hb_i][:, j, :], stgb[hb_i])
                        else:
                            k = pi[0] % 8
                            pi[0] += 1
                            ft_tile(ps[k], psb[k], w[wi_], wb_[wi_], j * 128, hTg[hb_i], hTgb[hb_i])
                            fnc = AF.Copy if kind == "copy" else AF.Sigmoid
                            act.op(lambda e, k=k, j=j, fnc=fnc: e.activation(out=stg[hb_i][:, j, :], in_=ps[k][:], func=fnc),
                                   reads=[psb[k]], writes=[stgb[hb_i]])
                    if kind == "copy":
                        pool.dma([(QTF[:, hh, 0:64, g * 512:(g + 1) * 512].rearrange("j d t -> d j t"),
                                   stg[hb_i][hh * 64:(hh + 1) * 64, :, :]) for hh in range(2)], reads=[stgb[hb_i]])
                    else:
                        pool.dma([(dst[t0:t0 + nt, :, g * 512:(g + 1) * 512].rearrange("c p t -> p c t"),
                                   stg[hb_i][:, 0:nt, :])], reads=[stgb[hb_i]])
            P.close()

        def run_pipeline(jobs, LOOK, front, back, tail_a, tail_b, DEFER=2):
            pend = []
            nj = len(jobs)
            n = 0
            while n < nj + LOOK or pend:
                if n < nj:
                    front(jobs[n], n)
                m = n - LOOK
                if 0 <= m < nj:
                    back(jobs[m], m)
                    if jobs[m]["last"]:
                        tail_a(jobs[m])
                        pend.append((n + DEFER, jobs[m]))
                while pend and (pend[0][0] <= n or n >= nj + LOOK):
                    tail_b(pend.pop(0)[1])
                n += 1

        if upto >= 4:
            P = Phase()
            kt, ktb = P.sb("kt", [65, TA], BF16, 2)
            qt, qtb = P.sb("qt", [65, TL], BF16, 2)
            vv, vvb = P.sb("vv", [128, NBA, 130], BF16, 2)
            ncs, ncsb = P.sb("ncs", [128, NBA * NH], F32)
            nct, nctb = P.sb("nct", [128, NBL * NH], F32)
            bia, biab = P.sb("bia", [128, NBA], F32, 2)
            tmk, tmkb = P.sb("tmk", [128, 128], BF16)
            fmk, fmkb = P.sb("fmk", [128, 2, 128], BF16)
            idt, idb = P.sb("idt", [128, 128], BF16)
            em, emb = P.sb("em", [65, 64], F32)
            pt, ptb = P.sb("pt", [128, 512], BF16, 4)
            osb, osbb = P.sb("osb", [65, 512], F32, 2)
            rinv, rinvb = P.sb("rinv", [64, 512], F32, 2)
            obT, obTb = P.sb("obT", [64, TL], BF16, 2)
            wc, wcb = P.sb("wc", [128, 16, 512], BF16, 2)
            psS, psSb = P.ps("psS", [128, 512], F32, 4)
            psO, psOb = P.ps("psO", [128, 512], F32, 2)
            psL, psLb = P.ps("psL", [128, 512], F32, 1)
            sp.dma([(ncs[:], NCS)], writes=[ncsb])
            sp.dma([(nct[:], NCT)], writes=[nctb])
            sp.dma([(tmk[:], tm_in)], writes=[tmkb])
            sp.dma([(fmk[:], fm_in)], writes=[fmkb])
            sp.dma([(idt[:], id_in)], writes=[idb])
            sp.dma([(em[:], em_in)], writes=[emb])
            chunks = [("u", sl) for sl in range(16)] + [("d", ch, hf) for ch in range(8) for hf in range(4)]
            cci = [0]

            def cast_chunk():
                if cci[0] >= len(chunks):
                    return
                cks = chunks[cci[0]]
                i = cci[0] % 2
                cci[0] += 1
                if cks[0] == "u":
                    sl = cks[1]
                    pool.dma([(wc[i][:, c:c + 4, :],
                               wu_in[c * 128:(c + 4) * 128, sl * 512:(sl + 1) * 512].rearrange("(c p) n -> p c n", p=128))
                              for c in range(0, 16, 4)], writes=[wcb[i]])
                    pool.dma([(WUB[sl], wc[i][:])], reads=[wcb[i]])
                else:
                    ch, hf = cks[1], cks[2]
                    r0 = hf * 16 * 128
                    pool.dma([(wc[i][:, c:c + 8, 0:256],
                               wd_in[r0 + c * 128:r0 + (c + 8) * 128, ch * 256:(ch + 1) * 256].rearrange("(c p) n -> p c n", p=128))
                              for c in range(0, 16, 8)], writes=[wcb[i]])
                    pool.dma([(WDB[ch, hf], wc[i][:, :, 0:256])], reads=[wcb[i]])

            NT = (NBL + 3) // 4
            jobs = []
            for h in range(NH):
                for T_ in range(NT):
                    t0 = 4 * T_
                    nq = min(4, NBL - t0)
                    tiles = [(s_, 0, None) for s_ in list(range(t0)) + list(range(NBL, NBL + t0))]
                    for d_ in range(nq):
                        tiles.append((t0 + d_, d_, tmk[:]))
                        tiles.append((NBL + t0 + d_, d_, fmk[:, (t0 + d_) % 2, :]))
                    g_ = h * NT + T_
                    for i_, (s_, d_, mk) in enumerate(tiles):
                        jobs.append(dict(h=h, T=T_, t0=t0, nq=nq, s=s_, d=d_, mk=mk, g=g_, first=(i_ == 0),
                                         last=(i_ == len(tiles) - 1), hfirst=(T_ == 0 and i_ == 0)))

            def loads(h):
                j, hh = h // 2, h % 2
                sp.dma([(kt[h % 2][:], KTF[j, hh])], writes=[ktb[h % 2]])
                sp.dma([(qt[h % 2][:], QTF[j, hh])], writes=[qtb[h % 2]])
                if hh == 0:
                    sp.dma([(vv[j % 2][:], VF[:, :, j * 130:(j + 1) * 130].rearrange("s p n -> p s n"))],
                           writes=[vvb[j % 2]])

            loads(0)

            def front(jb, n):
                h, t0, nq, s_, d_, mk, g_ = jb["h"], jb["t0"], jb["nq"], jb["s"], jb["d"], jb["mk"], jb["g"]
                hb_ = h % 2
                b_ = g_ % 2
                if jb["hfirst"] and h + 1 < NH:
                    loads(h + 1)
                if jb["first"]:
                    if g_ % 2 == 0:
                        cast_chunk()
                    rf = min(t0 + 2, NBL - 1)
                    dve.op(lambda e: e.tensor_scalar(
                        out=bia[b_][:], in0=ncs[:].rearrange("p (s h) -> p s h", h=NH)[:, :, h],
                        scalar1=nct[:, rf * NH + h:rf * NH + h + 1], scalar2=None, op0=ALU.add),
                        reads=[ncsb, nctb], writes=[biab[b_]])
                k = n % 4
                q0 = (t0 + d_) * 128
                N = (nq - d_) * 128
                fns = [lambda e: e.matmul(psS[k][:, 0:N], lhsT=kt[hb_][0:65, s_ * 128:(s_ + 1) * 128],
                                          rhs=qt[hb_][0:65, q0:q0 + N], start=True, stop=(mk is None))]
                rd = [ktb[hb_], qtb[hb_]]
                if mk is not None:
                    fns.append(lambda e: e.matmul(psS[k][:, 0:128], lhsT=idt[:], rhs=mk, start=False, stop=True))
                    rd += [idb, tmkb, fmkb]
                pe.group(fns, reads=rd, writes=[psSb[k]])
                act.op(lambda e: e.activation(out=pt[k][:, 0:N], in_=psS[k][:, 0:N], func=AF.Exp, scale=0.125,
                                              bias=bia[b_][:, s_:s_ + 1]), reads=[psSb[k], biab[b_]], writes=[ptb[k]])

            def back(jb, m):
                h, nq, s_, d_, g_ = jb["h"], jb["nq"], jb["s"], jb["d"], jb["g"]
                k = m % 4
                o_ = g_ % 2
                jb_ = (h // 2) % 2
                hh = h % 2
                N = (nq - d_) * 128
                pe.group([lambda e: e.matmul(psO[o_][0:65, d_ * 128:d_ * 128 + N],
                                             lhsT=vv[jb_][:, s_, hh * 65:(hh + 1) * 65], rhs=pt[k][:, 0:N],
                                             start=jb["first"], stop=jb["last"])],
                         reads=[ptb[k], vvb[jb_]], writes=[psOb[o_]])

            def tail_a(jb):
                o_ = jb["g"] % 2
                NQ = jb["nq"] * 128
                dve.op(lambda e: e.tensor_copy(out=osb[o_][:, 0:NQ], in_=psO[o_][0:65, 0:NQ]),
                       reads=[psOb[o_]], writes=[osbb[o_]])

            def tail_b(jb):
                h, t0, g_ = jb["h"], jb["t0"], jb["g"]
                o_ = g_ % 2
                hb_ = h % 2
                NQ = jb["nq"] * 128
                pe.group([lambda e: e.matmul(psL[0:64, 0:NQ], lhsT=em[:], rhs=osb[o_][:, 0:NQ], start=True, stop=True)],
                         reads=[emb, osbb[o_]], writes=[psLb])
                dve.op(lambda e: e.reciprocal(out=rinv[o_][:, 0:NQ], in_=psL[0:64, 0:NQ]),
                       reads=[psLb], writes=[rinvb[o_]])
                dve.op(lambda e: e.tensor_tensor(out=obT[hb_][:, t0 * 128:t0 * 128 + NQ], in0=osb[o_][0:64, 0:NQ],
                                                 in1=rinv[o_][:, 0:NQ], op=ALU.mult),
                       reads=[osbb[o_], rinvb[o_]], writes=[obTb[hb_]])
                if jb["T"] == NT - 1:
                    j, hh = h // 2, h % 2
                    pool.dma([(OBT[j, hh * 64:(hh + 1) * 64, :], obT[hb_][:])], reads=[obTb[hb_]])

            run_pipeline(jobs, 3, front, back, tail_a, tail_b)
            while cci[0] < len(chunks):
                cast_chunk()
            P.close()

        if upto >= 5:
            P = Phase()
            kt, ktb = P.sb("kt", [128, 2, TA], BF16)
            qt, qtb = P.sb("qt", [128, 4, TL], BF16)
            vv, vvb = P.sb("vv", [128, NBA, 130], BF16)
            m4, m4b = P.sb("m4", [128, 5, 512], BF16)
            idt, idb = P.sb("idt", [128, 128], BF16)
            em, emb = P.sb("em", [65, 64], F32)
            skt, skb = P.sb("skt", [64, NH], F32)
            skr, skrb = P.sb("skr", [64, NH, 128], F32)
            pt, ptb = P.sb("pt", [128, 512], BF16, 3)
            osb, osbb = P.sb("osb", [65, 512], F32, 2)
            rinv, rinvb = P.sb("rinv", [64, 512], F32, 2)
            oaT, oaTb = P.sb("oaT", [64, 4, TL], BF16, 2)
            psS, psSb = P.ps("psS", [128, 512], F32, 3)
            psO, psOb = P.ps("psO", [128, 512], F32, 2)
            psL, psLb = P.ps("psL", [128, 512], F32, 1)
            sp.dma([(kt[:], KTA.rearrange("c p t -> p c t"))], writes=[ktb])
            sp.dma([(vv[:], VA.rearrange("s p n -> p s n"))], writes=[vvb])
            sp.dma([(m4[:], m4_in)], writes=[m4b])
            sp.dma([(idt[:], id_in)], writes=[idb])
            sp.dma([(em[:], em_in)], writes=[emb])
            sp.dma([(skt[:], sk_in.broadcast_to([64, NH]))], writes=[skb])
            act.op(lambda e: e.activation(out=skt[:], in_=skt[:], func=AF.Exp), reads=[skb], writes=[skb])
            dve.op(lambda e: e.memset(skr[:], 0.0), writes=[skrb])
            for qd in range(4):
                for jj in range(4):
                    h = 8 * (qd // 2) + 2 * jj + (qd % 2)
                    dve.op(lambda e, h=h, qd=qd, jj=jj: e.tensor_scalar(
                        out=skr[:, qd * 4 + jj, :], in0=skr[:, qd * 4 + jj, :], scalar1=skt[:, h:h + 1],
                        scalar2=None, op0=ALU.add), reads=[skb], writes=[skrb])
            jobs = []
            for qd in range(4):
                for t in range(NBL):
                    p = t % 2
                    items = [(t, 0)]
                    if t > 0:
                        items.append((t - 1, 1 + 2 * p))
                    items.append((NBL + t, 2 + 2 * p))
                    for i_, (s_, mi) in enumerate(items):
                        jobs.append(dict(qd=qd, t=t, s=s_, mi=mi, g=qd * NBL + t, first=(i_ == 0),
                                         last=(i_ == len(items) - 1), qfirst=(t == 0 and i_ == 0)))

            def front(jb, n):
                qd, t, s_, mi = jb["qd"], jb["t"], jb["s"], jb["mi"]
                kv, hh = qd // 2, qd % 2
                pr = slice(hh * 64, hh * 64 + 64)
                if jb["qfirst"] and hh == 0:
                    sp.dma([(qt[:], QTA[4 * kv:4 * kv + 4].rearrange("c p t -> p c t"))], writes=[qtb])
                k = n % 3
                fns = [lambda e: e.matmul(psS[k][:].rearrange("p (a b) -> p a b", b=128),
                                          lhsT=kt[pr, kv, s_ * 128:(s_ + 1) * 128],
                                          rhs=qt[pr, :, t * 128:(t + 1) * 128], start=True, stop=False)]
                fns.append(lambda e: e.matmul(psS[k][:, :], lhsT=idt[:], rhs=m4[:, mi, :], start=False, stop=True))
                pe.group(fns, reads=[ktb, qtb, idb, m4b], writes=[psSb[k]])
                act.op(lambda e: e.activation(out=pt[k][:], in_=psS[k][:], func=AF.Exp, scale=0.125),
                       reads=[psSb[k]], writes=[ptb[k]])

            def back(jb, m):
                kv = jb["qd"] // 2
                k = m % 3
                o_ = jb["g"] % 2
                s_ = jb["s"]
                pe.group([lambda e: e.matmul(psO[o_][0:65, :], lhsT=vv[:, s_, kv * 65:(kv + 1) * 65], rhs=pt[k][:],
                                             start=jb["first"], stop=jb["last"])],
                         reads=[ptb[k], vvb], writes=[psOb[o_]])

            def tail_a(jb):
                o_ = jb["g"] % 2
                dve.op(lambda e: e.tensor_copy(out=osb[o_][:], in_=psO[o_][0:65, :]), reads=[psOb[o_]], writes=[osbb[o_]])

            def tail_b(jb):
                qd, t = jb["qd"], jb["t"]
                kv, hh = qd // 2, qd % 2
                qb_ = qd % 2
                o_ = jb["g"] % 2
                pe.group([lambda e: e.matmul(psL[0:64, :], lhsT=em[:], rhs=osb[o_][:], start=True, stop=True)],
                         reads=[emb, osbb[o_]], writes=[psLb])
                dve.op(lambda e: e.tensor_tensor(out=rinv[o_][:], in0=psL[0:64, :],
                                                 in1=skr[:, 4 * qd:4 * qd + 4, :].rearrange("p a b -> p (a b)"), op=ALU.add),
                       reads=[psLb, skrb], writes=[rinvb[o_]])
                act.op(lambda e: e.activation(out=rinv[o_][:], in_=rinv[o_][:], func=AF.Ln), reads=[rinvb[o_]], writes=[rinvb[o_]])
                act.op(lambda e: e.activation(out=rinv[o_][:], in_=rinv[o_][:], func=AF.Exp, scale=-1.0),
                       reads=[rinvb[o_]], writes=[rinvb[o_]])
                dve.op(lambda e: e.tensor_tensor(out=oaT[qb_][:, :, t * 128:(t + 1) * 128],
                                                 in0=osb[o_][0:64, :].rearrange("p (a b) -> p a b", b=128),
                                                 in1=rinv[o_][:].rearrange("p (a b) -> p a b", b=128), op=ALU.mult),
                       reads=[osbb[o_], rinvb[o_]], writes=[oaTb[qb_]])
                if t == NBL - 1:
                    pool.dma([(OAT[4 * kv + hq, hh * 64:hh * 64 + 64, :], oaT[qb_][:, hq, :]) for hq in range(4)],
                             reads=[oaTb[qb_]])

            run_pipeline(jobs, 2, front, back, tail_a, tail_b)
            P.close()

        if upto >= 6:
            P = Phase()
            wa, wab = P.sb("wa", [128, 8, D], BF16)
            wbt, wbb = P.sb("wb", [128, 8, D], BF16)
            oa, oab = P.sb("oa", [128, 8, 512], BF16, 2)
            obx, obxb = P.sb("obx", [128, 8, 512], BF16, 2)
            ga, gab = P.sb("ga", [128, 16, 512], BF16, 1)
            ga, gab = [ga, ga], [gab, gab]
            gbx, gbxb = P.sb("gbx", [128, 16, 512], BF16, 1)
            gbx, gbxb = [gbx, gbx], [gbxb, gbxb]
            m1, m1b = P.sb("m1", [128, 512], F32, 2)
            m2, m2b = P.sb("m2", [128, 512], F32, 2)
            mt, mtb = P.sb("mt", [128, 16, 512], BF16, 2)
            ps, psb = P.ps("ps", [128, 512], F32, 8)
            load_w_cast(wa, wab, wa_in, 2048, nchunk=8)
            load_w_cast(wbt, wbb, wb_in, 2048, nchunk=8)
            pi = 0
            for g in range(NGL):
                i = g % 2
                tsl = slice(g * 512, (g + 1) * 512)
                sp.dma([(oa[i][:], OAT[:, :, tsl].rearrange("c p t -> p c t"))], writes=[oab[i]])
                sp.dma([(obx[i][:], OBT[:, :, tsl].rearrange("c p t -> p c t"))], writes=[obxb[i]])
                sp.dma([(ga[i][:], GAT[:, :, tsl].rearrange("c p t -> p c t"))], writes=[gab[i]])
                sp.dma([(gbx[i][:], GBT[:, :, tsl].rearrange("c p t -> p c t"))], writes=[gbxb[i]])
                for dt in range(16):
                    k1 = pi % 8
                    k2 = (pi + 1) % 8
                    pi += 2
                    mi = dt % 2
                    pe.group([(lambda e, c=c, k1=k1: e.matmul(ps[k1][:], lhsT=wa[:, c, dt * 128:(dt + 1) * 128],
                                                              rhs=oa[i][:, c, :], start=(c == 0), stop=(c == 7)))
                              for c in range(8)], reads=[wab, oab[i]], writes=[psb[k1]])
                    pe.group([(lambda e, c=c, k2=k2: e.matmul(ps[k2][:], lhsT=wbt[:, c, dt * 128:(dt + 1) * 128],
                                                              rhs=obx[i][:, c, :], start=(c == 0), stop=(c == 7)))
                              for c in range(8)], reads=[wbb, obxb[i]], writes=[psb[k2]])
                    dve.op(lambda e, k1=k1, mi=mi, dt=dt: e.tensor_tensor(out=m1[mi][:], in0=ps[k1][:], in1=ga[i][:, dt, :],
                                                                          op=ALU.mult),
                           reads=[psb[k1], gab[i]], writes=[m1b[mi]])
                    dve.op(lambda e, k2=k2, mi=mi, dt=dt: e.tensor_tensor(out=m2[mi][:], in0=ps[k2][:], in1=gbx[i][:, dt, :],
                                                                          op=ALU.mult),
                           reads=[psb[k2], gbxb[i]], writes=[m2b[mi]])
                    pool.op(lambda e, mi=mi, dt=dt: e.tensor_tensor(out=mt[i][:, dt, :], in0=m1[mi][:], in1=m2[mi][:],
                                                                    op=ALU.add),
                            reads=[m1b[mi], m2b[mi]], writes=[mtb[i]])
                pool.dma([(MT[:, :, tsl].rearrange("c p t -> p c t"), mt[i][:])], reads=[mtb[i]])
            P.close()

        if upto >= 7:
            P = Phase()
            wo, wob = P.sb("wo", [128, 16, D], BF16)
            mt, mtb = P.sb("mt", [128, 16, 512], BF16, 1)
            mt, mtb = [mt, mt], [mtb, mtb]
            xt, xb = P.sb("xt", [128, D], F32, 2)
            x1, x1b = P.sb("x1", [128, D], F32, 2)
            gt, gb = P.sb("gt", [128, D], F32)
            ht, hb = P.sb("ht", [128, D], BF16, 2)
            junk, jb = P.sb("junk", [128, D], BF16)
            ss, ssb = P.sb("ss", [128, 1], F32, 2)
            epsc, epb = P.sb("epsc", [128, 1], F32)
            idt, idb = P.sb("idt", [128, 128], BF16)
            hTg, hTb = P.sb("hTg", [128, 16, 512], BF16, 2)
            ps, psb = P.ps("ps", [128, 512], F32, 4)
            pT, pTb = P.ps("pT", [128, 8, 128], BF16, 4)
            load_w_cast(wo, wob, wo_in, 2048)
            sp.dma([(gt[:], mn_in.broadcast_to([128, D]))], writes=[gb])
            sp.dma([(idt[:], id_in)], writes=[idb])
            dve.op(lambda e: e.memset(epsc[:], EPS), writes=[epb])
            cnt = [0]
            pi = 0
            pend = [None]
            for g in range(NGL):
                gi_ = g % 2
                sp.dma([(mt[gi_][:], MT[:, :, g * 512:(g + 1) * 512].rearrange("c p t -> p c t"))], writes=[mtb[gi_]])
                for bl in range(4):
                    s = g * 4 + bl
                    i = s % 2
                    sp.dma([(xt[i][:], x_in[s * 128:(s + 1) * 128, :])], writes=[xb[i]])
                    for cg in range(4):
                        k = pi % 4
                        pi += 1
                        pe.group([(lambda e, c=c, k=k: e.matmul(ps[k][:], lhsT=mt[gi_][:, c, bl * 128:(bl + 1) * 128],
                                                                rhs=wo[:, c, cg * 512:(cg + 1) * 512],
                                                                start=(c == 0), stop=(c == 15))) for c in range(16)],
                                 reads=[mtb[gi_], wob], writes=[psb[k]])
                        dve.op(lambda e, k=k, cg=cg: e.tensor_tensor(out=x1[i][:, cg * 512:(cg + 1) * 512], in0=ps[k][:],
                                                                     in1=xt[i][:, cg * 512:(cg + 1) * 512], op=ALU.add),
                               reads=[psb[k], xb[i]], writes=[x1b[i]])
                    if pend[0] is not None:
                        pend[0]()
                        pend[0] = None
                    pool.dma([(X1[s * 128:(s + 1) * 128, :], x1[i][:])], reads=[x1b[i]])
                    rmsnorm_to_bf16(P, x1[i], x1b[i], gt, gb, ht[i], hb[i], ss[i], ssb[i], junk, jb, epsc, epb)

                    def mk(i=i, gi_=gi_, bl=bl, g=g):
                        def f():
                            transpose16(ht[i], hb[i], idt, idb, pT, pTb, hTg[gi_], hTb[gi_], bl, cnt)
                            if bl == 3:
                                pool.dma([(H2T[:, :, g * 512:(g + 1) * 512].rearrange("c p t -> p c t"), hTg[gi_][:])],
                                         reads=[hTb[gi_]])
                        return f
                    pend[0] = mk()
            if pend[0] is not None:
                pend[0]()
            P.close()

        if upto >= 8:
            P = Phase()
            h2, h2b = P.sb("h2", [128, 16, 512], BF16)
            uT, uTb = P.sb("uT", [128, 64, 512], BF16)
            wu, wub = P.sb("wu", [128, 16, 256], BF16, 2)
            wd, wdb = P.sb("wd", [128, 8, 512], BF16, 3)
            rr, rrb = P.sb("rr", [128, 512], F32, 2)
            x2, x2b = P.sb("x2", [128, D], F32, 4)
            gt, gb = P.sb("gt", [128, D], F32)
            junk, jb = P.sb("junk", [128, 512], BF16)
            ss, ssb = P.sb("ss", [128, 4], F32, 2)
            s1, s1b = P.sb("s1", [128, 1], F32, 2)
            epsc, epb = P.sb("epsc", [128, 1], F32)
            ps, psb = P.ps("ps", [128, 512], F32, 4)
            pa, pab = P.ps("pa", [128, 512], F32, 4)
            sp.dma([(gt[:], fn_in.broadcast_to([128, D]))], writes=[gb])
            dve.op(lambda e: e.memset(epsc[:], EPS), writes=[epb])
            pi = 0
            wi_ = 0
            di_ = 0
            for g in range(NGL):
                sp.dma([(h2[:], H2T[:, :, g * 512:(g + 1) * 512].rearrange("c p t -> p c t"))], writes=[h2b])
                for bl in range(4):
                    s = g * 4 + bl
                    sp.dma([(x2[bl][:], X1[s * 128:(s + 1) * 128, :])], writes=[x2b[bl]])
                for sl in range(32):
                    w_ = wi_ % 2
                    wi_ += 1
                    sp.dma([(wu[w_][:], WUB[sl // 2][:, :, (sl % 2) * 256:(sl % 2) * 256 + 256])], writes=[wub[w_]])
                    for f4 in range(2):
                        ft = sl * 2 + f4
                        k = pi % 4
                        pi += 1
                        r_ = pi % 2
                        pe.group([(lambda e, c=c, k=k: e.matmul(ps[k][:], lhsT=wu[w_][:, c, f4 * 128:(f4 + 1) * 128],
                                                                rhs=h2[:, c, :], start=(c == 0), stop=(c == 15)))
                                  for c in range(16)], reads=[wub[w_], h2b], writes=[psb[k]])
                        act.op(lambda e, k=k, r_=r_: e.activation(out=rr[r_][:], in_=ps[k][:], func=AF.Relu),
                               reads=[psb[k]], writes=[rrb[r_]])
                        pool.op(lambda e, r_=r_, ft=ft: e.tensor_tensor(out=uT[:, ft, :], in0=rr[r_][:], in1=rr[r_][:],
                                                                        op=ALU.mult),
                                reads=[rrb[r_]], writes=[uTb])
                for ch in range(4):
                    for h8 in range(8):
                        d_ = di_ % 3
                        di_ += 1
                        hf, c0 = h8 // 2, (h8 % 2) * 8
                        sp.dma([(wd[d_][:, :, 0:256], WDB[2 * ch, hf][:, c0:c0 + 8, :]),
                                (wd[d_][:, :, 256:512], WDB[2 * ch + 1, hf][:, c0:c0 + 8, :])], writes=[wdb[d_]])
                        for bl in range(4):
                            fns = [(lambda e, c=c, bl=bl, h8=h8, d_=d_: e.matmul(
                                pa[bl][:, :], lhsT=uT[:, h8 * 8 + c, bl * 128:(bl + 1) * 128],
                                rhs=wd[d_][:, c, :], start=(h8 == 0 and c == 0), stop=(h8 == 7 and c == 7)))
                                for c in range(8)]
                            pe.group(fns, reads=[uTb, wdb[d_]], writes=[pab[bl]])
                    for bl in range(4):
                        dve.op(lambda e, bl=bl, ch=ch: e.tensor_tensor(out=x2[bl][:, ch * 512:(ch + 1) * 512],
                                                                       in0=pa[bl][:, :],
                                                                       in1=x2[bl][:, ch * 512:(ch + 1) * 512], op=ALU.add),
                               reads=[pab[bl], x2b[bl]], writes=[x2b[bl]])
                for bl in range(4):
                    s = g * 4 + bl
                    i = bl % 2
                    for q in range(4):
                        dve.op(lambda e, bl=bl, q=q, i=i: e.scalar_tensor_tensor(
                            out=junk[:], in0=x2[bl][:, q * 512:(q + 1) * 512], scalar=1.0,
                            in1=x2[bl][:, q * 512:(q + 1) * 512], op0=ALU.mult, op1=ALU.mult,
                            accum_out=ss[i][:, q:q + 1]), reads=[x2b[bl]], writes=[jb, ssb[i]])
                    dve.op(lambda e, i=i: e.tensor_reduce(out=s1[i][:], in_=ss[i][:], axis=mybir.AxisListType.X,
                                                          op=ALU.add), reads=[ssb[i]], writes=[s1b[i]])
                    act.op(lambda e, i=i: e.activation(out=s1[i][:], in_=s1[i][:], func=AF.Sqrt, scale=1.0 / D,
                                                       bias=epsc[:]), reads=[s1b[i], epb], writes=[s1b[i]])
                    dve.op(lambda e, i=i: e.reciprocal(out=s1[i][:], in_=s1[i][:]), reads=[s1b[i]], writes=[s1b[i]])
                    dve.op(lambda e, bl=bl, i=i: e.scalar_tensor_tensor(out=x2[bl][:], in0=x2[bl][:], scalar=s1[i][:],
                                                                        in1=gt[:], op0=ALU.mult, op1=ALU.mult),
                           reads=[x2b[bl], s1b[i], gb], writes=[x2b[bl]])
                    pool.dma([(y_out[s * 128:(s + 1) * 128, :], x2[bl][:])], reads=[x2b[bl]])
            P.close()
        if upto < 8:
            pass
    return nc


def _block_lists(nb_seq, typ):
    own, oth = [], []
    for j in range(nb_seq // 4):
        a = [4 * j, 4 * j + 3]
        b = [4 * j + 1, 4 * j + 2]
        own += a if typ == 0 else b
        oth += b if typ == 0 else a
    return own, oth


def _consts(typ):
    k = np.arange(128)[:, None]
    q = np.arange(128)[None, :]
    ident = np.eye(128, dtype=np.float32).astype(ml_dtypes.bfloat16)
    cmat = np.zeros((128, 4, 128), np.float32)
    cmat[:, 3, :] = np.eye(128)
    cmat[:, 0, :] = (k <= q)
    cmat[:, 1, :] = 1.0
    cmat[:, 2, :] = (k < 64)
    tri = np.where(k <= q, 0.0, NEG).astype(np.float32)
    band = np.where(k > q, 0.0, NEG).astype(np.float32)
    allm = np.full((128, 128), NEG, np.float32)
    fl = [(p ^ typ) for p in (0, 1)]
    swam = np.zeros((128, 2, 2, 128), np.float32)
    for p in (0, 1):
        swam[:, p, 0, :] = allm if fl[p] else band
        swam[:, p, 1, :] = band if fl[p] else allm
    flags = np.zeros((128, 6), np.float32)
    flags[:, 0], flags[:, 1] = fl[0], fl[1]
    flags[:, 2], flags[:, 3] = 1 - fl[0], 1 - fl[1]
    flags[:, 4], flags[:, 5] = (1 - fl[0]) * NEG, (1 - fl[1]) * NEG
    pidx = np.arange(128)
    invf = (10000.0 ** (-(np.arange(0, 64, 2, dtype=np.float64)) / 64.0))[pidx % 32]
    sgn = np.where((pidx % 64) < 32, -1.0, 1.0)
    ropec = np.stack([invf / (2 * np.pi), sgn * 2 * np.pi], axis=1).astype(np.float32)
    foxm = np.zeros((128, 2, 128), np.float32)
    for p in (0, 1):
        foxm[:, p, :] = 0.0 if fl[p] else NEG
    m5 = [tri, swam[:, 0, 0, :], swam[:, 0, 1, :], swam[:, 1, 0, :], swam[:, 1, 1, :]]
    swam4 = np.stack([np.tile(m_, (1, 4)) for m_ in m5], axis=1).astype(ml_dtypes.bfloat16)
    emat = np.zeros((65, 64), np.float32)
    emat[64, :] = 1.0
    return dict(ident=ident, cmat=cmat, foxmask=foxm.astype(ml_dtypes.bfloat16), emat=emat, swamask4=swam4, trimask=tri.astype(ml_dtypes.bfloat16),
                swamask=swam.astype(ml_dtypes.bfloat16), flags=flags, ropec=ropec)


def _perm_w_in(w_in):
    aq, ak, av = 0, 1024, 1152
    bq, bk, bv, bf = 1280, 2304, 3328, 4352
    ga, gb = 4368, 6416
    sw = np.concatenate([np.arange(32, 64), np.arange(0, 32)])
    cols_all = list(range(bk, bk + 1024))
    for kv in range(2):
        base = ak + kv * 64 + np.arange(64)
        bsw = ak + kv * 64 + sw
        cols_all += list(base) + list(base) + list(bsw) + list(bsw)
    cols_all += list(range(bv, bv + 1024)) + list(range(av, av + 128)) + list(range(bf, bf + 16))
    cols_loc = list(range(bq, bq + 1024))
    for j in range(8):
        t = aq + j * 128 + np.arange(128)
        ts = np.concatenate([aq + j * 128 + sw, aq + j * 128 + 64 + sw])
        cols_loc += list(t) + list(ts)
    cols_loc += list(range(ga, ga + 2048)) + list(range(gb, gb + 2048))
    return np.ascontiguousarray(w_in[:, cols_all]), np.ascontiguousarray(w_in[:, cols_loc])


def make_inputs(x, positions, attn_norm, w_in, fox_f_bias, swa_sinks, w_branch_swa, w_branch_fox, w_out,
                mlp_norm, w_up, w_down, final_norm, n_cores=8):
    B, S, _ = x.shape
    nb_seq = S // 128
    w_all, w_loc = _perm_w_in(np.asarray(w_in[0]))
    shared = dict(attn_norm=np.asarray(attn_norm[0:1]), w_all=w_all, w_loc=w_loc, fbias=np.asarray(fox_f_bias[0:1]),
                  sinks=np.asarray(swa_sinks[0:1]), wa=np.asarray(w_branch_swa[0]), wb=np.asarray(w_branch_fox[0]),
                  wout=np.asarray(w_out[0]), mlp_norm=np.asarray(mlp_norm[0:1]), w_up=np.asarray(w_up[0]),
                  w_down=np.asarray(w_down[0]), final_norm=np.asarray(final_norm).reshape(1, D))
    in_maps, owns = [], []
    for c in range(n_cores):
        b, typ = c // 2, c % 2
        own, oth = _block_lists(nb_seq, typ)
        order = own + oth
        xb = np.asarray(x[b]).reshape(nb_seq, 128, D)[order].reshape(S, D)
        pb = np.asarray(positions[b]).reshape(nb_seq, 128)[order].reshape(1, S).astype(np.int32)
        m = dict(shared)
        m.update(_consts(typ))
        m["x"] = np.ascontiguousarray(xb)
        m["pos"] = np.ascontiguousarray(pb)
        in_maps.append(m)
        owns.append((b, own))
    return in_maps, owns


_NC_CACHE = {}


def kernel(x, positions, attn_norm, w_in, fox_f_bias, swa_sinks, w_branch_swa, w_branch_fox, w_out,
           mlp_norm, w_up, w_down, final_norm):
    x = np.asarray(x)
    B, S, _ = x.shape
    NBL = S // 128 // 2
    in_maps, owns = make_inputs(x, positions, attn_norm, w_in, fox_f_bias, swa_sinks, w_branch_swa, w_branch_fox,
                                w_out, mlp_norm, w_up, w_down, final_norm)
    if NBL not in _NC_CACHE:
        _NC_CACHE[NBL] = build(NBL)
    res = run_bass_kernel_spmd(_NC_CACHE[NBL], in_maps, core_ids=list(range(8)))
    out = np.zeros((B, S, D), np.float32)
    o4 = out.reshape(B, S // 128, 128, D)
    for c, (b, own) in enumerate(owns):
        y = np.asarray(res.results[c]["y"]).reshape(NBL, 128, D)
        o4[b, own] = y
    return out
```
